# Optimizing a Trainium2 kernel written in Bass

```python
import jax
import jax.numpy as jnp
from jax import lax
import numpy as np


D_MODEL = 4096
BATCH = 1
SEQ = 16384
DEPTH = 2

D_A = D_MODEL // 2
A_GROUPS = 8
A_CHUNK = 128
D_B = D_MODEL // 2
POOL_WINDOWS = (2, 4, 8, 16)
B_GROUPS = len(POOL_WINDOWS)
D_C = D_MODEL // 2
CONV_WIDTH = 3
N_BRANCH = 3
D_FF = 256 * ((8 * D_MODEL + 3 * 256 - 1) // (3 * 256))
N_IN = 2 * D_A + D_B + 3 * D_C + N_BRANCH * D_MODEL
EPS = 1e-6

kernel_name = 'hybrid_gated_sgu_pool_shortconv'


def rms_norm(x, g):
    xf = x.astype(jnp.float32)
    xf = xf * lax.rsqrt(jnp.mean(xf * xf, axis=-1, keepdims=True) + EPS)
    return xf.astype(x.dtype) * g


def layer_norm(x, g, b):
    xf = x.astype(jnp.float32)
    mu = jnp.mean(xf, axis=-1, keepdims=True)
    var = jnp.mean(jnp.square(xf - mu), axis=-1, keepdims=True)
    return ((xf - mu) * lax.rsqrt(var + EPS)).astype(x.dtype) * g + b


def spatial_gating_mixer(z, ln_g, ln_b, w_s, b_s):
    bsz, s, _ = z.shape
    u, v = jnp.split(z, 2, axis=-1)
    v = layer_norm(v, ln_g, ln_b)
    vc = v.reshape(bsz, s // A_CHUNK, A_CHUNK, A_GROUPS, D_A // A_GROUPS)
    causal = jnp.tril(jnp.ones((A_CHUNK, A_CHUNK), dtype=bool))
    w = jnp.where(causal[None], w_s, jnp.zeros_like(w_s))
    mixed = jnp.einsum('gts,bnsgc->bntgc', w, vc) + b_s.T[:, :, None]
    return u * mixed.reshape(bsz, s, D_A)


def multiscale_pool_mixer(xb, w_pool, scale):
    bsz, s, _ = xb.shape
    dg = D_B // B_GROUPS
    xg = xb.astype(jnp.float32).reshape(bsz, s, B_GROUPS, dg)
    csum = jnp.cumsum(xg, axis=1)
    pos = jnp.arange(1, s + 1, dtype=jnp.float32)
    groups = []
    for g, win in enumerate(POOL_WINDOWS):
        c = csum[:, :, g]
        c_prev = jnp.pad(c, ((0, 0), (win, 0), (0, 0)))[:, :s]
        count = jnp.minimum(pos, float(win))[None, :, None]
        groups.append((c - c_prev) / count - xg[:, :, g])
    pooled = jnp.stack(groups, axis=2).astype(xb.dtype)
    y = jnp.einsum('bsgc,gcd->bsgd', pooled, w_pool).reshape(bsz, s, D_B)
    return y * scale


def short_conv_mixer(b_gate, c_gate, h, conv_w):
    z = c_gate * h
    z = lax.conv_general_dilated(
        z, conv_w[:, None, :].astype(z.dtype),
        window_strides=(1,), padding=[(CONV_WIDTH - 1, 0)],
        dimension_numbers=('NWC', 'WIO', 'NWC'), feature_group_count=D_C)
    return b_gate * z


def setup_inputs(seed: int = 0) -> dict:
    key = jax.random.key(seed)
    k = jax.random.split(key, 20)
    L = DEPTH
    dg = D_B // B_GROUPS

    def nrm(kk, shape, scale):
        return jax.random.normal(kk, shape, jnp.float32) * scale

    return {
        'x': nrm(k[0], (BATCH, SEQ, D_MODEL), 1.0),
        'norm_mix_g': 1.0 + nrm(k[1], (L, D_MODEL), 0.02),
        'w_in': nrm(k[2], (L, D_MODEL, N_IN), D_MODEL ** -0.5),
        'ln_a_g': 1.0 + nrm(k[3], (L, D_A), 0.02),
        'ln_a_b': nrm(k[4], (L, D_A), 0.02),
        'w_spatial': nrm(k[5], (L, A_GROUPS, A_CHUNK, A_CHUNK), A_CHUNK ** -0.5),
        'b_spatial': 1.0 + nrm(k[6], (L, A_GROUPS, A_CHUNK), 0.02),
        'w_pool': nrm(k[7], (L, B_GROUPS, dg, dg), dg ** -0.5),
        'pool_scale': 1.0 + nrm(k[8], (L, D_B), 0.02),
        'conv_w': nrm(k[9], (L, CONV_WIDTH, D_C), CONV_WIDTH ** -0.5),
        'w_branch_a': nrm(k[10], (L, D_A, D_MODEL), D_A ** -0.5),
        'w_branch_b': nrm(k[11], (L, D_B, D_MODEL), D_B ** -0.5),
        'w_branch_c': nrm(k[12], (L, D_C, D_MODEL), D_C ** -0.5),
        'w_out': nrm(k[13], (L, D_MODEL, D_MODEL), D_MODEL ** -0.5),
        'norm_ffn_g': 1.0 + nrm(k[14], (L, D_MODEL), 0.02),
        'w_ffn_gate': nrm(k[15], (L, D_MODEL, D_FF), D_MODEL ** -0.5),
        'w_ffn_up': nrm(k[16], (L, D_MODEL, D_FF), D_MODEL ** -0.5),
        'w_ffn_down': nrm(k[17], (L, D_FF, D_MODEL), D_FF ** -0.5),
        'final_norm_g': 1.0 + nrm(k[18], (D_MODEL,), 0.02),
    }


def reference(x, norm_mix_g, w_in, ln_a_g, ln_a_b, w_spatial, b_spatial, w_pool,
              pool_scale, conv_w, w_branch_a, w_branch_b, w_branch_c, w_out,
              norm_ffn_g, w_ffn_gate, w_ffn_up, w_ffn_down, final_norm_g):
    bsz, s, _ = x.shape
    cuts = [2 * D_A,
            2 * D_A + D_B,
            2 * D_A + D_B + D_C,
            2 * D_A + D_B + 2 * D_C,
            2 * D_A + D_B + 3 * D_C]
    for l in range(DEPTH):
        h = rms_norm(x, norm_mix_g[l])
        proj = jnp.einsum('bsd,dn->bsn', h, w_in[l])
        z_a, x_b, b_c, c_c, h_c, gate_logits = jnp.split(proj, cuts, axis=-1)
        y_a = spatial_gating_mixer(jax.nn.gelu(z_a), ln_a_g[l], ln_a_b[l],
                                   w_spatial[l], b_spatial[l])
        y_b = multiscale_pool_mixer(x_b, w_pool[l], pool_scale[l])
        y_c = short_conv_mixer(b_c, c_c, h_c, conv_w[l])
        gates = jax.nn.sigmoid(gate_logits.astype(jnp.float32)).astype(x.dtype)
        gates = gates.reshape(bsz, s, N_BRANCH, D_MODEL)
        merged = (gates[:, :, 0] * (y_a @ w_branch_a[l])
                  + gates[:, :, 1] * (y_b @ w_branch_b[l])
                  + gates[:, :, 2] * (y_c @ w_branch_c[l]))
        x = x + merged @ w_out[l]
        h = rms_norm(x, norm_ffn_g[l])
        ff = jax.nn.silu(h @ w_ffn_gate[l]) * (h @ w_ffn_up[l])
        x = x + ff @ w_ffn_down[l]
    return rms_norm(x, final_norm_g)
```

```python
import os
import numpy as np
import concourse.bass as bass
import concourse.mybir as mybir
from concourse.bass_utils import run_bass_kernel_spmd

F32 = mybir.dt.float32
BF16 = mybir.dt.bfloat16
AF = mybir.ActivationFunctionType
ALU = mybir.AluOpType
AX = mybir.AxisListType

EPS = 1e-6
ENGS = ("pe", "act", "dve", "pool", "sp")


class Cfg:
    def __init__(self, D=4096, DFF=11008, L=2, tiles=(512, 512, 384, 384, 384), ncores=8, nslot=5, stop=None):
        self.stop = stop
        import os
        self.sub = int(os.environ.get('KSUB', '99'))
        self.nocache = os.environ.get('KNOCACHE', '0') == '1'
        self.D, self.DFF, self.L = D, DFF, L
        self.tiles = tuple(tiles)
        self.ncores = ncores
        self.nslot = nslot
        self.KD = D // 128
        self.DA = D // 2
        self.KA = self.DA // 128
        self.AG = self.DA // 256
        self.DB = D // 2
        self.KB = self.DB // 128
        self.CB = self.KB // 4
        self.DC = D // 2
        self.KC = self.DC // 128
        self.NIN = 2 * self.DA + self.DB + 3 * self.DC + 3 * D
        self.U0 = 0
        self.V0 = self.DA
        self.XB0 = 2 * self.DA
        self.BC0 = self.XB0 + self.DB
        self.CC0 = self.BC0 + self.DC
        self.HC0 = self.CC0 + self.DC
        self.G0 = self.HC0 + self.DC
        self.KF = DFF // 128
        self.TMAX = max(tiles)
        self.NPRE = 128
        self.C0 = 96
        self.NTOK = sum(tiles) - 128
        self.CV_GMIX = 0
        self.CV_GFFN = self.KD
        self.CV_PSC = 2 * self.KD
        self.CV_CONV = 2 * self.KD + self.KB
        self.CV_L = 2 * self.KD + self.KB + 3 * self.KC
        self.CV_FIN = self.L * self.CV_L
        self.NCV = self.CV_FIN + self.KD


class Buf:
    __slots__ = ("name", "lw", "rd")

    def __init__(self, name):
        self.name = name
        self.lw = None
        self.rd = {}


class Sched:
    def __init__(self):
        self.ops = {e: [] for e in ENGS}
        self.cnt = {}
        self.known = {e: {} for e in ENGS}
        self.nwait = 0
        self.pending = {}

    def _emit(self, eng, fn, reads, writes, sem, inc, ident, selfwait=False):
        need = {}

        def add(s, v):
            if self.known[eng].get(s, 0) >= v:
                return
            if need.get(s, 0) < v:
                need[s] = v

        if selfwait and self.cnt.get(sem, 0) > 0:
            add(sem, self.cnt[sem])
        for (ps_, pv_) in self.pending.pop(eng, ()):
            add(ps_, pv_)
        for b in reads:
            if b.lw is not None:
                add(b.lw[0], b.lw[1])
        for b in writes:
            if b.lw is not None and (ident != "pe" or b.lw[2] != ident):
                add(b.lw[0], b.lw[1])
            for s, (v, e) in b.rd.items():
                if ident != "pe" or e != ident:
                    add(s, v)
        for s, v in need.items():
            self.known[eng][s] = v
        self.nwait += len(need)
        val = self.cnt.get(sem, 0) + inc
        self.cnt[sem] = val
        for b in writes:
            b.lw = (sem, val, ident)
            b.rd = {}
        for b in reads:
            b.rd[sem] = (val, ident)
        self.ops[eng].append((tuple(need.items()), fn, sem, inc))
        return val

    def op(self, eng, fn, reads=(), writes=()):
        return self._emit(eng, fn, reads, writes, "c_" + eng, 1, eng)

    def dma(self, queue, fn, sem, reads=(), writes=()):
        return self._emit(queue, fn, reads, writes, sem, 16, None, selfwait=True)


class Reg:
    def __init__(self, ap, bufs, per_chunk):
        self.ap = ap
        self.bufs = bufs
        self.per_chunk = per_chunk

    def b(self, i=None, j=None):
        if not self.per_chunk or i is None:
            if not self.per_chunk:
                return list(self.bufs)
            out = []
            for x in self.bufs:
                for y in x:
                    if y not in out:
                        out.append(y)
            return out
        if j is None:
            return list(self.bufs[i])
        out = []
        for x in self.bufs[i:j]:
            for y in x:
                if y not in out:
                    out.append(y)
        return out


GRAN = 2048


def build(cfg):
    c = cfg
    D, KD, L = c.D, c.KD, c.L
    TM = c.TMAX
    nc = bass.Bass("TRN2", target_bir_lowering=False)

    def din(name, shape):
        return nc.dram_tensor(name, list(shape), F32, kind="ExternalInput").ap()

    x_d = din("x", [c.NPRE + c.NTOK, D])
    w_in_d = din("w_in", [L, D, c.NIN])
    w_pool_d = din("w_pool", [L, 4, c.DB // 4, c.DB // 4])
    w_br_d = [din("w_branch_a", [L, c.DA, D]), din("w_branch_b", [L, c.DB, D]), din("w_branch_c", [L, c.DC, D])]
    w_out_d = din("w_out", [L, D, D])
    w_gate_d = din("w_ffn_gate", [L, D, c.DFF])
    w_up_d = din("w_ffn_up", [L, D, c.DFF])
    w_down_d = din("w_ffn_down", [L, c.DFF, D])
    wsT_d = din("wsT", [L, c.AG, 128, 128])
    cvec_d = din("cvec", [128, c.NCV])
    lng_d = din("ln_a_g", [L, c.DA])
    lnb_d = din("ln_a_b", [L, c.DA])
    bsp_d = din("b_spatial", [L, c.AG * 128])
    consts_d = din("consts", [128, 3, 128])
    invc_d = din("invc", [128, 4, c.tiles[0]])
    y_d = nc.dram_tensor("y", [c.NTOK, D], F32, kind="ExternalOutput").ap()
    NSEQ = (c.NIN // 512) * (KD // 4) + 4 + 3 * (D // 512) * (c.KA // 4) + (D // 512) * (D // 512) \
        + 2 * ((c.KF + 3) // 4 + 2) * (KD // 4) + 2 * (D // 512) * (((c.KF + 1) // 2 + 3) // 4) + 8
    TPP = 448
    NPART = (NSEQ + TPP - 1) // TPP
    wsc_parts = [[nc.dram_tensor("wsc_%d_%d" % (l_, p_), [min(TPP, NSEQ - p_ * TPP), 128, 2048], BF16,
                                 kind="Internal").ap() for p_ in range(NPART)] for l_ in range(L)]

    S = Sched()
    NMISC = 8
    sem_names = ["c_pe", "c_act", "c_dve", "c_pool", "d_x0", "d_x1", "d_o0", "d_o1"] + \
                ["d_m%d" % i for i in range(NMISC)] + ["d_w%d" % i for i in range(c.nslot)] + \
                ["d_b%d" % i for i in range(c.nslot)]
    misc_i = [0]

    def msem():
        misc_i[0] += 1
        return "d_m%d" % (misc_i[0] % NMISC)

    import contextlib
    es = contextlib.ExitStack()

    def sb(name, shape, dt):
        return es.enter_context(nc.sbuf_tensor(name, list(shape), dt))

    xs_t = sb("xs", [128, KD, TM], F32)
    h_t = sb("h", [128, KD, TM], BF16)
    ring_t = [sb("ring%d" % i, [128, 4, 512], BF16) for i in range(c.nslot)]
    wmT_t = sb("wmT", [128, L * c.AG, 128], BF16)
    cvec_t = sb("cvec_sb", [128, c.NCV], F32)
    consts_t = sb("consts_sb", [128, 3, 128], F32)
    xbh_t = sb("xbhalo", [128, L * c.KB, 16], F32)
    zh_t = sb("zhalo", [128, L * c.KC, 2], F32)
    rstd_t = sb("rstd", [128, TM], F32)
    stat_t = sb("stat", [128, 64], F32)

    ARENA = max(3 * c.KA * TM * 2 + 3 * 4 * TM * 4 + 2048, 32 * 1024)
    ARENA = (ARENA + GRAN - 1) // GRAN * GRAN
    arena_t = sb("arena", [128, ARENA // 2], BF16)
    gran = [Buf("g%d" % i) for i in range(ARENA // GRAN)]

    ps_t = [es.enter_context(nc.psum_tensor("ps%d" % i, [128, 512], F32)) for i in range(8)]
    ps = [Reg(ps_t[i], [Buf("ps%d" % i)], False) for i in range(8)]

    xs = Reg(xs_t, [[Buf("xs%d" % k)] for k in range(KD)], True)
    hh = Reg(h_t, [[Buf("h%d" % k)] for k in range(KD)], True)
    ring = [Reg(ring_t[i], [Buf("ring%d" % i)], False) for i in range(c.nslot)]
    wmT = Reg(wmT_t, [Buf("wmT")], False)
    cvec = Reg(cvec_t, [Buf("cvec")], False)
    consts = Reg(consts_t, [Buf("consts")], False)
    xbh = Reg(xbh_t, [[Buf("xbh%d" % k)] for k in range(L * c.KB)], True)
    zh = Reg(zh_t, [[Buf("zh%d" % k)] for k in range(L * c.KC)], True)
    rstd = Reg(rstd_t, [Buf("rstd")], False)
    stat = Reg(stat_t, [Buf("stat")], False)
    statA = Reg(stat_t, [Buf("statA")], False)

    class Arena:
        def __init__(self):
            self.off = 0

        def reset(self):
            self.off = 0

        def alloc(self, shape, dt, per_chunk=False):
            esz = 4 if dt == F32 else 2
            n = 1
            for s_ in shape:
                n *= s_
            nbytes = n * esz
            off = (self.off + 31) // 32 * 32
            assert off + nbytes <= ARENA, ("arena overflow", off, nbytes, ARENA)
            self.off = off + nbytes
            ap = arena_t[:, off // 2:(off + nbytes) // 2]
            if dt == F32:
                ap = ap.bitcast(F32)
            if len(shape) == 2:
                ap = ap.rearrange("p (a b) -> p a b", b=shape[1])
            elif len(shape) == 3:
                ap = ap.rearrange("p (a b c) -> p a b c", b=shape[1], c=shape[2])
            if per_chunk:
                cb = nbytes // shape[0]
                bufs = []
                for i in range(shape[0]):
                    lo, hi = off + i * cb, off + (i + 1) * cb
                    bufs.append(gran[lo // GRAN:(hi - 1) // GRAN + 1])
                return Reg(ap, bufs, True)
            return Reg(ap, gran[off // GRAN:(off + nbytes - 1) // GRAN + 1], False)

    AR = Arena()

    ident = consts_t[:, 0, :]
    cmask = consts_t[:, 1, :]
    ones = consts_t[:, 2, :]

    job_ctr = [0]

    def next_half():
        hsel = job_ctr[0] % 2
        job_ctr[0] += 1
        return [ps[hsel * 4 + i] for i in range(4)]

    ring_i = [0]

    cur = {"ti": 0, "l": 0, "seq": 0, "C0": 0}

    def wfetch(src_ap, nk, ncols):
        slot_i = ring_i[0] % c.nslot
        ring_i[0] += 1
        slot = ring[slot_i]
        dst = slot.ap[:, 0:nk, 0:ncols]
        l_, seq, ti_ = cur["l"], cur["seq"], cur["ti"]
        cur["seq"] += 1
        assert seq < NSEQ, seq
        cache = wsc_parts[l_][seq // TPP][seq % TPP].rearrange("p (k c) -> p k c", c=512)[:, 0:nk, 0:ncols]
        wbt = l_ + (seq % 2)
        if wbt >= len(c.tiles) - 1 or c.nocache:
            wbt = 1 << 30
        if ti_ > wbt:
            S.dma("pool", lambda e, dst=dst, src=cache: e.dma_start(out=dst, in_=src),
                  "d_w%d" % slot_i, writes=slot.b())
        else:
            S.dma("pool", lambda e, dst=dst, src=src_ap: e.dma_start(out=dst, in_=src),
                  "d_w%d" % slot_i, writes=slot.b())
            if ti_ == wbt:
                S.dma("sp", lambda e, dst=dst, cache=cache: e.dma_start(out=cache, in_=dst),
                      "d_b%d" % slot_i, reads=slot.b())
        return slot

    def wsrc(w_ap2d, r0, nk, c0, ncols):
        return w_ap2d[r0:r0 + nk * 128, c0:c0 + ncols].rearrange("(k p) c -> p k c", p=128)

    def fm_job(w2d, c0, ncols, rhs_reg, nkchunks, T, row0=0):
        nm = ncols // 128
        banks = next_half()
        nkt = (nkchunks + 3) // 4
        roff = cur["C0"] if rhs_reg is hh else 0
        for kt in range(nkt):
            nk = min(4, nkchunks - kt * 4)
            slot = wfetch(wsrc(w2d, row0 + kt * 512, nk, c0, ncols), nk, ncols)

            def fn(e, slot=slot, kt=kt, nk=nk):
                ins = None
                for m in range(nm):
                    for k in range(nk):
                        kk = kt * 4 + k
                        ins = e.matmul(banks[m].ap[:, 0:T], slot.ap[:, k, m * 128:(m + 1) * 128],
                                       rhs_reg.ap[:, kk, roff:roff + T],
                                       start=(kk == 0), stop=(kk == nkchunks - 1))
                return ins
            S.op("pe", fn, reads=slot.b() + rhs_reg.b(kt * 4, kt * 4 + nk),
                 writes=[bk.bufs[0] for bk in banks[:nm]])
        return banks[:nm]

    def act(fn, reads, writes):
        S.op("act", fn, reads, writes)

    def dve(fn, reads, writes):
        S.op("dve", fn, reads, writes)

    S.dma("sp", lambda e: e.dma_start(out=consts_t[:, :, :], in_=consts_d[:, :, :]), msem(), writes=consts.b())
    S.dma("sp", lambda e: e.dma_start(out=cvec_t[:, :], in_=cvec_d[:, :]), msem(), writes=cvec.b())
    dve(lambda e: e.memset(xbh_t[:, :, :], 0.0), [], xbh.b())
    dve(lambda e: e.memset(zh_t[:, :, :], 0.0), [], zh.b())
    AR.reset()
    wst = AR.alloc([L * c.AG, 128], F32)
    S.dma("sp", lambda e: e.dma_start(out=wst.ap, in_=wsT_d.rearrange("l g s t -> s (l g) t")), msem(),
          writes=wst.b())
    for lg in range(L * c.AG):
        dve(lambda e, lg=lg: e.tensor_tensor(out=wmT_t[:, lg, :], in0=wst.ap[:, lg, :], in1=cmask, op=ALU.mult),
            wst.b() + consts.b(), wmT.b())

    def load_x(tok0, T):
        nt = T // 128
        AR.reset()
        stg = [AR.alloc([1024], F32), AR.alloc([1024], F32)]
        nslab = D // 1024
        i = 0
        for tc in range(nt):
            for s_ in range(nslab):
                st = stg[i % 2]
                semn = "d_x%d" % (i % 2)
                S.dma("sp", lambda e, st=st, tc=tc, s_=s_: e.dma_start(
                    out=st.ap, in_=x_d[tok0 + tc * 128: tok0 + (tc + 1) * 128, s_ * 1024:(s_ + 1) * 1024]),
                    semn, writes=st.b())
                banks = next_half()
                for b2 in range(2):
                    def fn(e, st=st, b2=b2, banks=banks):
                        ins = None
                        for q in range(4):
                            ins = e.transpose(banks[b2].ap[:, q * 128:(q + 1) * 128],
                                              st.ap[:, (b2 * 4 + q) * 128:(b2 * 4 + q + 1) * 128], ident)
                        return ins
                    S.op("pe", fn, reads=st.b() + consts.b(), writes=banks[b2].b())
                    k0 = s_ * 8 + b2 * 4
                    src = banks[b2].ap[:, :].rearrange("p (a b) -> p a b", b=128)
                    dst = xs_t[:, k0:k0 + 4, tc * 128:(tc + 1) * 128]
                    if b2 == 0:
                        act(lambda e, dst=dst, src=src: e.activation(out=dst, in_=src, func=AF.Copy),
                            banks[b2].b(), xs.b(k0, k0 + 4))
                    else:
                        dve(lambda e, dst=dst, src=src: e.tensor_copy(out=dst, in_=src),
                            banks[b2].b(), xs.b(k0, k0 + 4))
                i += 1

    def rms_stats(T):
        sq = [AR.alloc([T], F32), AR.alloc([T], F32)]
        bank = next_half()[0]
        for k in range(KD):
            q = sq[k % 2]
            act(lambda e, q=q, k=k: e.activation(out=q.ap, in_=xs_t[:, k, 0:T], func=AF.Square),
                xs.b(k), q.b())
            S.op("pe", lambda e, q=q, k=k: e.matmul(bank.ap[:, 0:T], ones, q.ap, start=(k == 0), stop=(k == KD - 1)),
                 reads=q.b() + consts.b(), writes=bank.b())
        act(lambda e: e.activation(out=rstd_t[:, 0:T], in_=bank.ap[:, 0:T], func=AF.Sqrt, scale=1.0 / D, bias=EPS),
            bank.b(), rstd.b())
        dve(lambda e: e.reciprocal(out=rstd_t[:, 0:T], in_=rstd_t[:, 0:T]), rstd.b(), rstd.b())

    def rmsnorm_to_h(T, gcol0):
        rms_stats(T)
        for k in range(KD):
            dve(lambda e, k=k: e.scalar_tensor_tensor(out=h_t[:, k, 0:T], in0=xs_t[:, k, 0:T],
                                                      scalar=cvec_t[:, gcol0 + k:gcol0 + k + 1],
                                                      in1=rstd_t[:, 0:T], op0=ALU.mult, op1=ALU.mult),
                xs.b(k) + cvec.b() + rstd.b(), hh.b(k))

    def branch_a(l, T):
        nt = T // 128
        NFG = c.DA // 512
        win = w_in_d[l]
        C0 = cur["C0"]
        Tj = T - C0
        AR.reset()
        y_a = AR.alloc([c.KA, Tj], BF16, per_chunk=True)
        mark = AR.off
        v = AR.alloc([nt, c.DA], BF16, per_chunk=True)
        tmp = AR.alloc([c.DA], F32)
        lng = AR.alloc([c.DA], F32)
        lnb = AR.alloc([c.DA], F32)
        bsb = AR.alloc([c.AG, 128], F32)
        u_sb = AR.alloc([4, Tj], F32, per_chunk=True)
        tmp2s = [AR.alloc([T], F32), AR.alloc([T], F32)]
        S.dma("sp", lambda e: e.dma_start(out=lng.ap, in_=lng_d[l].partition_broadcast(128)), msem(), writes=lng.b())
        S.dma("sp", lambda e: e.dma_start(out=lnb.ap, in_=lnb_d[l].partition_broadcast(128)), msem(), writes=lnb.b())
        S.dma("sp", lambda e: e.dma_start(out=bsb.ap.rearrange("p a b -> p (a b)"),
                                          in_=bsp_d[l].partition_broadcast(128)), msem(), writes=bsb.b())
        if c.sub < 1:
            return y_a, mark
        for fg in range(NFG):
            banks = next_half()
            c0 = c.V0 + fg * 512
            for kt in range(KD // 4):
                slot = wfetch(wsrc(win, kt * 512, 4, c0, 512), 4, 512)

                def fn(e, slot=slot, kt=kt, banks=banks):
                    ins = None
                    for tc in range(nt):
                        for k in range(4):
                            kk = kt * 4 + k
                            ins = e.matmul(banks[tc].ap[:, :], h_t[:, kk, tc * 128:(tc + 1) * 128], slot.ap[:, k, :],
                                           start=(kk == 0), stop=(kk == KD - 1))
                    return ins
                S.op("pe", fn, reads=slot.b() + hh.b(kt * 4, kt * 4 + 4), writes=[bk.bufs[0] for bk in banks[:nt]])
            if c.sub < 2:
                continue
            for tc in range(nt):
                col = tc * NFG + fg
                vdst = v.ap[:, tc, fg * 512:(fg + 1) * 512]
                act(lambda e, tc=tc, vdst=vdst, banks=banks: e.activation(
                    out=vdst, in_=banks[tc].ap[:, :], func=AF.Gelu_apprx_tanh), banks[tc].b(), v.b(tc))
                dve(lambda e, vdst=vdst, col=col: e.tensor_reduce(out=stat_t[:, col:col + 1], in_=vdst,
                                                                  axis=AX.X, op=ALU.add), v.b(tc), statA.b())
                dve(lambda e, vdst=vdst, col=col: e.tensor_tensor(out=tmp.ap[:, 0:512], in0=vdst, in1=vdst,
                                                                  op=ALU.mult), v.b(tc), tmp.b())
                dve(lambda e, col=col: e.tensor_reduce(out=stat_t[:, 16 + col:17 + col], in_=tmp.ap[:, 0:512],
                                                       axis=AX.X, op=ALU.add), tmp.b(), stat.b())
        if c.sub < 3:
            return y_a, mark
        NS = nt * NFG
        dve(lambda e: e.tensor_reduce(out=stat_t[:, 32:32 + nt],
                                      in_=stat_t[:, 0:NS].rearrange("p (a b) -> p a b", b=NFG),
                                      axis=AX.X, op=ALU.add), statA.b(), stat.b())
        dve(lambda e: e.tensor_reduce(out=stat_t[:, 36:36 + nt],
                                      in_=stat_t[:, 16:16 + NS].rearrange("p (a b) -> p a b", b=NFG),
                                      axis=AX.X, op=ALU.add), stat.b(), stat.b())
        dve(lambda e: e.tensor_scalar(out=stat_t[:, 40:40 + nt], in0=stat_t[:, 32:32 + nt], scalar1=1.0 / c.DA,
                                      scalar2=None, op0=ALU.mult), stat.b(), stat.b())
        dve(lambda e: e.tensor_tensor(out=stat_t[:, 44:44 + nt], in0=stat_t[:, 40:40 + nt], in1=stat_t[:, 40:40 + nt],
                                      op=ALU.mult), stat.b(), stat.b())
        dve(lambda e: e.scalar_tensor_tensor(out=stat_t[:, 44:44 + nt], in0=stat_t[:, 36:36 + nt], scalar=1.0 / c.DA,
                                             in1=stat_t[:, 44:44 + nt], op0=ALU.mult, op1=ALU.subtract),
            stat.b(), stat.b())
        act(lambda e: e.activation(out=stat_t[:, 48:48 + nt], in_=stat_t[:, 44:44 + nt], func=AF.Sqrt, scale=1.0, bias=EPS),
            stat.b(), stat.b())
        dve(lambda e: e.reciprocal(out=stat_t[:, 48:48 + nt], in_=stat_t[:, 48:48 + nt]), stat.b(), stat.b())
        if c.sub < 4:
            return y_a, mark
        for tc in range(nt):
            dve(lambda e, tc=tc: e.tensor_scalar(out=tmp.ap, in0=v.ap[:, tc, :], scalar1=stat_t[:, 40 + tc:41 + tc],
                                                 scalar2=stat_t[:, 48 + tc:49 + tc], op0=ALU.subtract, op1=ALU.mult),
                v.b(tc) + stat.b(), tmp.b())
            if c.sub < 6:
                continue
            dve(lambda e: e.tensor_tensor(out=tmp.ap, in0=tmp.ap, in1=lng.ap, op=ALU.mult), tmp.b() + lng.b(), tmp.b())
            if c.sub < 7:
                continue
            dve(lambda e, tc=tc: e.tensor_tensor(out=v.ap[:, tc, :], in0=tmp.ap, in1=lnb.ap, op=ALU.add),
                tmp.b() + lnb.b(), v.b(tc))
        if c.sub < 8:
            return y_a, mark
        for cgu in range(c.DA // 512):
            banks = fm_job(win, c.U0 + cgu * 512, 512, hh, KD, Tj)
            for m in range(4):
                act(lambda e, m=m, banks=banks: e.activation(out=u_sb.ap[:, m, :], in_=banks[m].ap[:, 0:Tj],
                                                             func=AF.Gelu_apprx_tanh), banks[m].b(), u_sb.b(m))
            mb = next_half()
            for m in range(4):
                j = cgu * 4 + m
                g = j // 2

                def fn(e, m=m, j=j, g=g, mb=mb):
                    ins = None
                    for tc in range(nt):
                        ins = e.matmul(mb[m].ap[:, tc * 128:(tc + 1) * 128], v.ap[:, tc, j * 128:(j + 1) * 128],
                                       wmT_t[:, l * c.AG + g, :], start=True, stop=True)
                    return ins
                S.op("pe", fn, reads=v.b() + wmT.b(), writes=mb[m].b())
                t2 = tmp2s[m % 2]
                dve(lambda e, m=m, g=g, mb=mb, t2=t2: e.tensor_tensor(
                    out=t2.ap.rearrange("p (a b) -> p a b", b=128),
                    in0=mb[m].ap[:, 0:T].rearrange("p (a b) -> p a b", b=128),
                    in1=bsb.ap[:, g:g + 1, :].to_broadcast([128, nt, 128]), op=ALU.add),
                    mb[m].b() + bsb.b(), t2.b())
                dve(lambda e, m=m, j=j, t2=t2: e.tensor_tensor(out=y_a.ap[:, j, :], in0=t2.ap[:, C0:T], in1=u_sb.ap[:, m, :],
                                                               op=ALU.mult), t2.b() + u_sb.b(m), y_a.b(j))
        return y_a, mark

    def branch_b(l, T, y_b, use_tab, halo_only):
        win = w_in_d[l]
        CB = c.CB
        W16 = T + 16
        xbuf = [AR.alloc([W16], F32) for _ in range(4)]
        pa = [AR.alloc([W16], F32) for _ in range(4)]
        pb = [AR.alloc([W16], F32) for _ in range(4)]
        pooled = [AR.alloc([CB, T], BF16, per_chunk=True) for _ in range(4)]
        tab = None
        if use_tab and not halo_only:
            tab = AR.alloc([4, T], F32)
            S.dma("sp", lambda e, C0=cur["C0"]: e.dma_start(out=tab.ap, in_=invc_d[:, :, C0:C0 + T]), msem(), writes=tab.b())
        pend = []
        for jj in range(c.DB // 512):
            while len(pend) > 1:
                pend.pop(0)()
            banks = fm_job(win, c.XB0 + jj * 512, 512, hh, KD, T)
            if pend:
                pend.pop(0)()
            for m in range(4):
                j = jj * 4 + m
                xb_ = xbuf[m]
                hidx = l * c.KB + j
                act(lambda e, xb_=xb_, hidx=hidx: e.activation(out=xb_.ap[:, 0:16], in_=xbh_t[:, hidx, :], func=AF.Copy),
                    xbh.b(hidx), xb_.b())
                act(lambda e, xb_=xb_, m=m, banks=banks: e.activation(out=xb_.ap[:, 16:16 + T], in_=banks[m].ap[:, 0:T],
                                                                    func=AF.Copy), banks[m].b(), xb_.b())
                act(lambda e, xb_=xb_, hidx=hidx: e.activation(out=xbh_t[:, hidx, :], in_=xb_.ap[:, T:T + 16], func=AF.Copy),
                    xb_.b(), xbh.b(hidx))
            if halo_only:
                continue
            srcs = [xbuf[m] for m in range(4)]
            los = [0, 0, 0, 0]
            gs = [(jj * 4 + m) // CB for m in range(4)]
            for stp in range(max(gs) + 1):
                for m in range(4):
                    if stp > gs[m]:
                        continue
                    sh = 1 << stp
                    dst = pa[m] if stp % 2 == 0 else pb[m]
                    src = srcs[m]
                    lo2 = los[m] + sh
                    dve(lambda e, src=src, dst=dst, lo2=lo2, sh=sh: e.tensor_tensor(
                        out=dst.ap[:, lo2:W16], in0=src.ap[:, lo2:W16], in1=src.ap[:, lo2 - sh:W16 - sh], op=ALU.add),
                        src.b(), dst.b())
                    srcs[m] = dst
                    los[m] = lo2
            if tab is not None:
                for m in range(4):
                    dve(lambda e, src=srcs[m], g=gs[m]: e.tensor_tensor(out=src.ap[:, 16:W16], in0=src.ap[:, 16:W16],
                                                                        in1=tab.ap[:, g, :], op=ALU.mult),
                        srcs[m].b() + tab.b(), srcs[m].b())
            for m in range(4):
                j = jj * 4 + m
                g = gs[m]
                src = srcs[m]
                xb_ = xbuf[m]
                pl = pooled[g % 4]
                wn = float(1 << (g + 1))
                if tab is None:
                    dve(lambda e, src=src, xb_=xb_, pl=pl, j=j, wn=wn: e.scalar_tensor_tensor(
                        out=pl.ap[:, j % CB, :], in0=src.ap[:, 16:W16], scalar=1.0 / wn, in1=xb_.ap[:, 16:W16],
                        op0=ALU.mult, op1=ALU.subtract), src.b() + xb_.b(), pl.b(j % CB))
                else:
                    dve(lambda e, src=src, xb_=xb_, pl=pl, j=j: e.tensor_tensor(
                        out=pl.ap[:, j % CB, :], in0=src.ap[:, 16:W16], in1=xb_.ap[:, 16:W16], op=ALU.subtract),
                        src.b() + xb_.b(), pl.b(j % CB))
                if (j + 1) % CB == 0:
                    def emit_pool(g=g, pl=pl):
                        pbanks = next_half()
                        slot = wfetch(wsrc(w_pool_d[l, g], 0, CB, 0, CB * 128), CB, CB * 128)

                        def fn(e, slot=slot, pl=pl, pbanks=pbanks):
                            ins = None
                            for mm in range(CB):
                                for k in range(CB):
                                    ins = e.matmul(pbanks[mm].ap[:, 0:T], slot.ap[:, k, mm * 128:(mm + 1) * 128],
                                                   pl.ap[:, k, :], start=(k == 0), stop=(k == CB - 1))
                            return ins
                        S.op("pe", fn, reads=slot.b() + pl.b(), writes=[bk.bufs[0] for bk in pbanks[:CB]])
                        for mm in range(CB):
                            jo = g * CB + mm
                            col = l * c.CV_L + c.CV_PSC + jo
                            act(lambda e, mm=mm, jo=jo, col=col, pbanks=pbanks: e.activation(
                                out=y_b.ap[:, jo, :], in_=pbanks[mm].ap[:, 0:T], func=AF.Copy,
                                scale=cvec_t[:, col:col + 1]), pbanks[mm].b() + cvec.b(), y_b.b(jo))
                    if os.environ.get('KNODEFER', '0') == '1':
                        emit_pool()
                    else:
                        pend.append(emit_pool)
        while len(pend) > 1:
            pend.pop(0)()
        return pend

    def branch_c(l, T, y_c, halo_only, pre_jobs=()):
        pre_jobs = list(pre_jobs)
        win = w_in_d[l]
        cc_sb = AR.alloc([4, T], F32, per_chunk=True)
        z = AR.alloc([4, T + 2], F32, per_chunk=True)
        acc = AR.alloc([4, T], F32, per_chunk=True)
        for jj in range(c.DC // 512):
            banks = fm_job(win, c.CC0 + jj * 512, 512, hh, KD, T)
            while pre_jobs:
                pre_jobs.pop(0)()
            for m in range(4):
                act(lambda e, m=m, banks=banks: e.activation(out=cc_sb.ap[:, m, :], in_=banks[m].ap[:, 0:T], func=AF.Copy),
                    banks[m].b(), cc_sb.b(m))
            banks = fm_job(win, c.HC0 + jj * 512, 512, hh, KD, T)
            for m in range(4):
                j = jj * 4 + m
                hidx = l * c.KC + j
                act(lambda e, m=m, hidx=hidx: e.activation(out=z.ap[:, m, 0:2], in_=zh_t[:, hidx, :], func=AF.Copy),
                    zh.b(hidx), z.b(m))
                dve(lambda e, m=m, banks=banks: e.tensor_tensor(out=z.ap[:, m, 2:2 + T], in0=cc_sb.ap[:, m, :],
                                                                in1=banks[m].ap[:, 0:T], op=ALU.mult),
                    cc_sb.b(m) + banks[m].b(), z.b(m))
                act(lambda e, m=m, hidx=hidx: e.activation(out=zh_t[:, hidx, :], in_=z.ap[:, m, T:T + 2], func=AF.Copy),
                    z.b(m), zh.b(hidx))
                if halo_only:
                    continue
                cw = l * c.CV_L + c.CV_CONV
                dve(lambda e, m=m, j=j, cw=cw: e.tensor_scalar(
                    out=acc.ap[:, m, :], in0=z.ap[:, m, 2:2 + T], scalar1=cvec_t[:, cw + 2 * c.KC + j:cw + 2 * c.KC + j + 1],
                    scalar2=None, op0=ALU.mult), z.b(m) + cvec.b(), acc.b(m))
                dve(lambda e, m=m, j=j, cw=cw: e.scalar_tensor_tensor(
                    out=acc.ap[:, m, :], in0=z.ap[:, m, 1:1 + T], scalar=cvec_t[:, cw + c.KC + j:cw + c.KC + j + 1],
                    in1=acc.ap[:, m, :], op0=ALU.mult, op1=ALU.add), z.b(m) + cvec.b() + acc.b(m), acc.b(m))
                dve(lambda e, m=m, j=j, cw=cw: e.scalar_tensor_tensor(
                    out=acc.ap[:, m, :], in0=z.ap[:, m, 0:T], scalar=cvec_t[:, cw + j:cw + j + 1],
                    in1=acc.ap[:, m, :], op0=ALU.mult, op1=ALU.add), z.b(m) + cvec.b() + acc.b(m), acc.b(m))
            if halo_only:
                continue
            banks = fm_job(win, c.BC0 + jj * 512, 512, hh, KD, T)
            for m in range(4):
                j = jj * 4 + m
                dve(lambda e, m=m, j=j, banks=banks: e.tensor_tensor(out=y_c.ap[:, j, :], in0=banks[m].ap[:, 0:T],
                                                                    in1=acc.ap[:, m, :], op=ALU.mult),
                    banks[m].b() + acc.b(m), y_c.b(j))

    def merge_and_out(l, T, ys):
        win = w_in_d[l]
        sg = AR.alloc([4, T], F32, per_chunk=True)
        acc = AR.alloc([4, T], F32, per_chunk=True)
        mg = [AR.alloc([4, T], BF16, per_chunk=True), AR.alloc([4, T], BF16, per_chunk=True)]
        pend_out = []

        def emit_out(cg):
            for n in range(D // 512):
                banks = fm_job(w_out_d[l], n * 512, 512, mg[cg % 2], 4, T, row0=cg * 512)
                for m in range(4):
                    kx = n * 4 + m
                    dve(lambda e, m=m, kx=kx, banks=banks, C0=cur["C0"]: e.tensor_tensor(
                        out=xs_t[:, kx, C0:C0 + T], in0=xs_t[:, kx, C0:C0 + T], in1=banks[m].ap[:, 0:T], op=ALU.add),
                        xs.b(kx) + banks[m].b(), xs.b(kx))

        for cg in range(D // 512):
            for br in range(3):
                banks = fm_job(win, c.G0 + br * D + cg * 512, 512, hh, KD, T)
                for m in range(4):
                    act(lambda e, m=m, banks=banks: e.activation(out=sg.ap[:, m, :], in_=banks[m].ap[:, 0:T],
                                                                 func=AF.Sigmoid), banks[m].b(), sg.b(m))
                if br == 0 and pend_out:
                    emit_out(pend_out.pop(0))
                banks = fm_job(w_br_d[br][l], cg * 512, 512, ys[br], c.KA, T)
                for m in range(4):
                    if br == 0:
                        dve(lambda e, m=m, banks=banks: e.tensor_tensor(out=acc.ap[:, m, :], in0=sg.ap[:, m, :],
                                                                        in1=banks[m].ap[:, 0:T], op=ALU.mult),
                            sg.b(m) + banks[m].b(), acc.b(m))
                    else:
                        dve(lambda e, m=m, banks=banks: e.tensor_tensor(out=sg.ap[:, m, :], in0=sg.ap[:, m, :],
                                                                        in1=banks[m].ap[:, 0:T], op=ALU.mult),
                            sg.b(m) + banks[m].b(), sg.b(m))
                        if br == 1:
                            dve(lambda e, m=m: e.tensor_tensor(out=acc.ap[:, m, :], in0=acc.ap[:, m, :],
                                                               in1=sg.ap[:, m, :], op=ALU.add),
                                sg.b(m) + acc.b(m), acc.b(m))
                        else:
                            dve(lambda e, m=m, mgc=mg[cg % 2]: e.tensor_tensor(out=mgc.ap[:, m, :], in0=acc.ap[:, m, :],
                                                                              in1=sg.ap[:, m, :], op=ALU.add),
                                sg.b(m) + acc.b(m), mg[cg % 2].b(m))
            pend_out.append(cg)
        while pend_out:
            emit_out(pend_out.pop(0))

    def ffn(l, T):
        AR.reset()
        halves = [(0, (c.KF + 1) // 2), ((c.KF + 1) // 2, c.KF)]
        ff = AR.alloc([(c.KF + 1) // 2, T], BF16, per_chunk=True)
        sgf = AR.alloc([4, T], F32, per_chunk=True)
        for (f0, f1) in halves:
            nf = f1 - f0
            fi = 0
            while fi < nf:
                nm = min(4, nf - fi)
                c0 = (f0 + fi) * 128
                banks = fm_job(w_gate_d[l], c0, nm * 128, hh, KD, T)
                for m in range(nm):
                    act(lambda e, m=m, banks=banks: e.activation(out=sgf.ap[:, m, :], in_=banks[m].ap[:, 0:T],
                                                                 func=AF.Silu), banks[m].b(), sgf.b(m))
                banks = fm_job(w_up_d[l], c0, nm * 128, hh, KD, T)
                for m in range(nm):
                    dve(lambda e, m=m, fi=fi, banks=banks: e.tensor_tensor(out=ff.ap[:, fi + m, :], in0=sgf.ap[:, m, :],
                                                                          in1=banks[m].ap[:, 0:T], op=ALU.mult),
                        sgf.b(m) + banks[m].b(), ff.b(fi + m))
                fi += nm
            for n in range(D // 512):
                banks = fm_job(w_down_d[l], n * 512, 512, ff, nf, T, row0=f0 * 128)
                for m in range(4):
                    kx = n * 4 + m
                    dve(lambda e, m=m, kx=kx, banks=banks, C0=cur["C0"]: e.tensor_tensor(
                        out=xs_t[:, kx, C0:C0 + T], in0=xs_t[:, kx, C0:C0 + T], in1=banks[m].ap[:, 0:T], op=ALU.add),
                        xs.b(kx) + banks[m].b(), xs.b(kx))

    def final_store(tok0_, T, skip):
        nt = T // 128
        tok_out0 = tok0_ - c.NPRE
        AR.reset()
        rms_stats(T)
        osl = AR.alloc([8, T], F32, per_chunk=True)
        ost = [AR.alloc([1024], F32), AR.alloc([1024], F32)]
        i = 0
        for s_ in range(D // 1024):
            for k in range(8):
                kk = s_ * 8 + k
                dve(lambda e, k=k, kk=kk: e.scalar_tensor_tensor(out=osl.ap[:, k, :], in0=xs_t[:, kk, 0:T],
                                                                 scalar=cvec_t[:, c.CV_FIN + kk:c.CV_FIN + kk + 1],
                                                                 in1=rstd_t[:, 0:T], op0=ALU.mult, op1=ALU.mult),
                    xs.b(kk) + cvec.b() + rstd.b(), osl.b(k))
            for tc in range(skip, nt):
                st = ost[i % 2]
                semn = "d_o%d" % (i % 2)
                banks = next_half()
                for b2 in range(2):
                    def fn(e, b2=b2, tc=tc, banks=banks):
                        ins = None
                        for q in range(4):
                            ins = e.transpose(banks[b2].ap[:, q * 128:(q + 1) * 128],
                                              osl.ap[:, b2 * 4 + q, tc * 128:(tc + 1) * 128], ident)
                        return ins
                    S.op("pe", fn, reads=osl.b(b2 * 4, b2 * 4 + 4) + consts.b(), writes=banks[b2].b())
                    if b2 == 0:
                        act(lambda e, st=st, banks=banks: e.activation(out=st.ap[:, 0:512], in_=banks[0].ap[:, :],
                                                                       func=AF.Copy), banks[0].b(), st.b())
                    else:
                        dve(lambda e, st=st, banks=banks: e.tensor_copy(out=st.ap[:, 512:1024], in_=banks[1].ap[:, :]),
                            banks[1].b(), st.b())
                S.dma("sp", lambda e, st=st, tc=tc, s_=s_: e.dma_start(
                    out=y_d[tok_out0 + tc * 128: tok_out0 + (tc + 1) * 128, s_ * 1024:(s_ + 1) * 1024], in_=st.ap),
                    semn, reads=st.b())
                i += 1

    STAGES = ("load", "norm", "A", "B", "C", "M", "ffn", None)

    def upto(name):
        return STAGES.index(c.stop) >= STAGES.index(name)

    def layer(l, T, first_tile):
        cv = l * c.CV_L
        AR.reset()
        if not upto("norm"):
            return
        rmsnorm_to_h(T, cv + c.CV_GMIX)
        if not upto("A"):
            return
        y_a, mark = branch_a(l, T)
        Tfull = T
        T = T - cur["C0"]
        AR.off = mark
        y_b = AR.alloc([c.KB, T], BF16, per_chunk=True)
        mark_b = AR.off
        if not upto("B"):
            return
        pend_b = branch_b(l, T, y_b, first_tile, False)
        AR.off = mark_b
        y_c = AR.alloc([c.KC, T], BF16, per_chunk=True)
        mark_c = AR.off
        if not upto("C"):
            for f_ in pend_b:
                f_()
            return
        branch_c(l, T, y_c, False, pend_b)
        AR.off = mark_c
        if not upto("M"):
            return
        merge_and_out(l, T, [y_a, y_b, y_c])
        AR.reset()
        if not upto("ffn"):
            return
        rmsnorm_to_h(Tfull, cv + c.CV_GFFN)
        ffn(l, T)

    tok0 = 0
    for ti, T in enumerate(c.tiles):
        cur["ti"] = ti
        cur["C0"] = c.C0 if ti == 0 else 0
        if ti >= 1:
            S.pending["pool"] = [("d_b%d" % i, S.cnt["d_b%d" % i]) for i in range(c.nslot) if ("d_b%d" % i) in S.cnt]
        load_x(tok0, T)
        for l in range(L):
            cur["l"] = l
            cur["seq"] = 0
            layer(l, T, ti == 0)
        final_store(tok0, T, 1 if ti == 0 else 0)
        tok0 += T

    final_waits = [(s_, S.cnt[s_]) for s_ in ("d_o0", "d_o1") if s_ in S.cnt]

    sem_ctx = {n: es.enter_context(nc.semaphore(n)) for n in sem_names}
    block = es.enter_context(nc.Block())

    def replay(engname, e):
        for waits, fn, sem, inc in S.ops[engname]:
            for (s_, v) in waits:
                e.wait_ge(sem_ctx[s_], v)
            ins = fn(e)
            ins.then_inc(sem_ctx[sem], inc)

    @block.tensor
    def _(e):
        replay("pe", e)

    @block.scalar
    def _(e):
        replay("act", e)

    @block.vector
    def _(e):
        replay("dve", e)

    @block.gpsimd
    def _(e):
        replay("pool", e)

    @block.sync
    def _(e):
        replay("sp", e)
        for (s_, v) in final_waits:
            e.wait_ge(sem_ctx[s_], v)

    es.close()
    return nc, S


def host_inputs(cfg, inp, core):
    c = cfg
    x = inp["x"].reshape(-1, c.D)
    per = c.NTOK
    lo = core * per - c.NPRE
    if lo < 0:
        xc = np.concatenate([np.zeros((c.NPRE, c.D), np.float32), x[0:per]], axis=0)
    else:
        xc = np.ascontiguousarray(x[lo:lo + c.NPRE + per])
    m = {"x": xc}
    return m


def shared_inputs(cfg, inp):
    c = cfg
    L = c.L
    sh = {}
    for k in ("w_in", "w_pool", "w_branch_a", "w_branch_b", "w_branch_c", "w_out",
              "w_ffn_gate", "w_ffn_up", "w_ffn_down", "ln_a_g", "ln_a_b"):
        sh[k] = np.ascontiguousarray(inp[k], dtype=np.float32)
    sh["wsT"] = np.ascontiguousarray(np.transpose(inp["w_spatial"], (0, 1, 3, 2)))
    sh["b_spatial"] = np.ascontiguousarray(inp["b_spatial"].reshape(L, -1))
    cv = np.zeros((128, c.NCV), np.float32)
    for l in range(L):
        b = l * c.CV_L
        cv[:, b + c.CV_GMIX:b + c.CV_GMIX + c.KD] = inp["norm_mix_g"][l].reshape(c.KD, 128).T
        cv[:, b + c.CV_GFFN:b + c.CV_GFFN + c.KD] = inp["norm_ffn_g"][l].reshape(c.KD, 128).T
        cv[:, b + c.CV_PSC:b + c.CV_PSC + c.KB] = inp["pool_scale"][l].reshape(c.KB, 128).T
        for j in range(3):
            cv[:, b + c.CV_CONV + j * c.KC:b + c.CV_CONV + (j + 1) * c.KC] = inp["conv_w"][l, j].reshape(c.KC, 128).T
    cv[:, c.CV_FIN:c.CV_FIN + c.KD] = inp["final_norm_g"].reshape(c.KD, 128).T
    sh["cvec"] = cv
    cs = np.zeros((128, 3, 128), np.float32)
    cs[:, 0, :] = np.eye(128, dtype=np.float32)
    cs[:, 1, :] = np.triu(np.ones((128, 128), np.float32))
    cs[:, 2, :] = 1.0
    sh["consts"] = cs
    return sh


def invc_table(cfg, core):
    c = cfg
    T0 = c.tiles[0]
    pos = core * c.NTOK + np.arange(1 - c.NPRE, T0 - c.NPRE + 1, dtype=np.float32)
    pos = np.maximum(pos, 1.0)
    tab = np.zeros((128, 4, T0), np.float32)
    for g in range(4):
        win = float(2 ** (g + 1))
        tab[:, g, :] = (np.float32(1.0) / np.minimum(pos, win))[None, :]
    return tab


_CACHE = {}


def run(cfg, inp):
    key = (cfg.D, cfg.DFF, cfg.tiles, cfg.ncores)
    if key not in _CACHE:
        _CACHE[key] = build(cfg)[0]
    nc = _CACHE[key]
    sh = shared_inputs(cfg, inp)
    in_maps = []
    for core in range(cfg.ncores):
        m = dict(sh)
        m.update(host_inputs(cfg, inp, core))
        m["invc"] = invc_table(cfg, core)
        in_maps.append(m)
    res = run_bass_kernel_spmd(nc, in_maps, core_ids=list(range(cfg.ncores)))
    out = np.concatenate([r["y"] for r in res.results], axis=0)
    return out


def kernel(**inputs):
    cfg = Cfg()
    inp = {k: np.asarray(v) for k, v in inputs.items()}
    out = run(cfg, inp)
    return out.reshape(1, cfg.ncores * cfg.NTOK, cfg.D).astype(np.float32, copy=False)
```

```python
import os
import numpy as np
import concourse.bass as bass
import concourse.mybir as mybir
from concourse.bass_utils import run_bass_kernel_spmd

F32 = mybir.dt.float32
BF16 = mybir.dt.bfloat16
AF = mybir.ActivationFunctionType
ALU = mybir.AluOpType
AX = mybir.AxisListType

EPS = 1e-6
ENGS = ("pe", "act", "dve", "pool", "sp")


class Cfg:
    def __init__(self, D=4096, DFF=11008, L=2, tiles=(512, 512, 384, 384, 384), ncores=8, nslot=5, stop=None):
        self.stop = stop
        import os
        self.sub = int(os.environ.get('KSUB', '99'))
        self.nocache = os.environ.get('KNOCACHE', '0') == '1'
        self.D, self.DFF, self.L = D, DFF, L
        self.tiles = tuple(tiles)
        self.ncores = ncores
        self.nslot = nslot
        self.KD = D // 128
        self.DA = D // 2
        self.KA = self.DA // 128
        self.AG = self.DA // 256
        self.DB = D // 2
        self.KB = self.DB // 128
        self.CB = self.KB // 4
        self.DC = D // 2
        self.KC = self.DC // 128
        self.NIN = 2 * self.DA + self.DB + 3 * self.DC + 3 * D
        self.U0 = 0
        self.V0 = self.DA
        self.XB0 = 2 * self.DA
        self.BC0 = self.XB0 + self.DB
        self.CC0 = self.BC0 + self.DC
        self.HC0 = self.CC0 + self.DC
        self.G0 = self.HC0 + self.DC
        self.KF = DFF // 128
        self.TMAX = max(tiles)
        self.NPRE = 128
        self.C0 = 96
        self.NTOK = sum(tiles) - 128
        self.CV_GMIX = 0
        self.CV_GFFN = self.KD
        self.CV_PSC = 2 * self.KD
        self.CV_CONV = 2 * self.KD + self.KB
        self.CV_L = 2 * self.KD + self.KB + 3 * self.KC
        self.CV_FIN = self.L * self.CV_L
        self.NCV = self.CV_FIN + self.KD


class Buf:
    __slots__ = ("name", "lw", "rd")

    def __init__(self, name):
        self.name = name
        self.lw = None
        self.rd = {}


class Sched:
    def __init__(self):
        self.ops = {e: [] for e in ENGS}
        self.cnt = {}
        self.known = {e: {} for e in ENGS}
        self.nwait = 0
        self.pending = {}

    def _emit(self, eng, fn, reads, writes, sem, inc, ident, selfwait=False):
        need = {}

        def add(s, v):
            if self.known[eng].get(s, 0) >= v:
                return
            if need.get(s, 0) < v:
                need[s] = v

        if selfwait and self.cnt.get(sem, 0) > 0:
            add(sem, self.cnt[sem])
        for (ps_, pv_) in self.pending.pop(eng, ()):
            add(ps_, pv_)
        for b in reads:
            if b.lw is not None:
                add(b.lw[0], b.lw[1])
        for b in writes:
            if b.lw is not None and (ident != "pe" or b.lw[2] != ident):
                add(b.lw[0], b.lw[1])
            for s, (v, e) in b.rd.items():
                if ident != "pe" or e != ident:
                    add(s, v)
        for s, v in need.items():
            self.known[eng][s] = v
        self.nwait += len(need)
        val = self.cnt.get(sem, 0) + inc
        self.cnt[sem] = val
        for b in writes:
            b.lw = (sem, val, ident)
            b.rd = {}
        for b in reads:
            b.rd[sem] = (val, ident)
        self.ops[eng].append((tuple(need.items()), fn, sem, inc))
        return val

    def op(self, eng, fn, reads=(), writes=()):
        return self._emit(eng, fn, reads, writes, "c_" + eng, 1, eng)

    def dma(self, queue, fn, sem, reads=(), writes=()):
        return self._emit(queue, fn, reads, writes, sem, 16, None, selfwait=True)


class Reg:
    def __init__(self, ap, bufs, per_chunk):
        self.ap = ap
        self.bufs = bufs
        self.per_chunk = per_chunk

    def b(self, i=None, j=None):
        if not self.per_chunk or i is None:
            if not self.per_chunk:
                return list(self.bufs)
            out = []
            for x in self.bufs:
                for y in x:
                    if y not in out:
                        out.append(y)
            return out
        if j is None:
            return list(self.bufs[i])
        out = []
        for x in self.bufs[i:j]:
            for y in x:
                if y not in out:
                    out.append(y)
        return out


GRAN = 2048


def build(cfg):
    c = cfg
    D, KD, L = c.D, c.KD, c.L
    TM = c.TMAX
    nc = bass.Bass("TRN2", target_bir_lowering=False)

    def din(name, shape):
        return nc.dram_tensor(name, list(shape), F32, kind="ExternalInput").ap()

    x_d = din("x", [c.NPRE + c.NTOK, D])
    w_in_d = din("w_in", [L, D, c.NIN])
    w_pool_d = din("w_pool", [L, 4, c.DB // 4, c.DB // 4])
    w_br_d = [din("w_branch_a", [L, c.DA, D]), din("w_branch_b", [L, c.DB, D]), din("w_branch_c", [L, c.DC, D])]
    w_out_d = din("w_out", [L, D, D])
    w_gate_d = din("w_ffn_gate", [L, D, c.DFF])
    w_up_d = din("w_ffn_up", [L, D, c.DFF])
    w_down_d = din("w_ffn_down", [L, c.DFF, D])
    wsT_d = din("wsT", [L, c.AG, 128, 128])
    cvec_d = din("cvec", [128, c.NCV])
    lng_d = din("ln_a_g", [L, c.DA])
    lnb_d = din("ln_a_b", [L, c.DA])
    bsp_d = din("b_spatial", [L, c.AG * 128])
    consts_d = din("consts", [128, 3, 128])
    invc_d = din("invc", [128, 4, c.tiles[0]])
    y_d = nc.dram_tensor("y", [c.NTOK, D], F32, kind="ExternalOutput").ap()
    NSEQ = (c.NIN // 512) * (KD // 4) + 4 + 3 * (D // 512) * (c.KA // 4) + (D // 512) * (D // 512) \
        + 2 * ((c.KF + 3) // 4 + 2) * (KD // 4) + 2 * (D // 512) * (((c.KF + 1) // 2 + 3) // 4) + 8
    TPP = 448
    NPART = (NSEQ + TPP - 1) // TPP
    wsc_parts = [[nc.dram_tensor("wsc_%d_%d" % (l_, p_), [min(TPP, NSEQ - p_ * TPP), 128, 2048], BF16,
                                 kind="Internal").ap() for p_ in range(NPART)] for l_ in range(L)]

    S = Sched()
    NMISC = 8
    sem_names = ["c_pe", "c_act", "c_dve", "c_pool", "d_x0", "d_x1", "d_o0", "d_o1"] + \
                ["d_m%d" % i for i in range(NMISC)] + ["d_w%d" % i for i in range(c.nslot)] + \
                ["d_b%d" % i for i in range(c.nslot)]
    misc_i = [0]

    def msem():
        misc_i[0] += 1
        return "d_m%d" % (misc_i[0] % NMISC)

    import contextlib
    es = contextlib.ExitStack()

    def sb(name, shape, dt):
        return es.enter_context(nc.sbuf_tensor(name, list(shape), dt))

    xs_t = sb("xs", [128, KD, TM], F32)
    h_t = sb("h", [128, KD, TM], BF16)
    ring_t = [sb("ring%d" % i, [128, 4, 512], BF16) for i in range(c.nslot)]
    wmT_t = sb("wmT", [128, L * c.AG, 128], BF16)
    cvec_t = sb("cvec_sb", [128, c.NCV], F32)
    consts_t = sb("consts_sb", [128, 3, 128], F32)
    xbh_t = sb("xbhalo", [128, L * c.KB, 16], F32)
    zh_t = sb("zhalo", [128, L * c.KC, 2], F32)
    rstd_t = sb("rstd", [128, TM], F32)
    stat_t = sb("stat", [128, 64], F32)

    ARENA = max(3 * c.KA * TM * 2 + 3 * 4 * TM * 4 + 2048, 48 * 1024)
    ARENA = (ARENA + GRAN - 1) // GRAN * GRAN
    arena_t = sb("arena", [128, ARENA // 2], BF16)
    gran = [Buf("g%d" % i) for i in range(ARENA // GRAN)]

    ps_t = [es.enter_context(nc.psum_tensor("ps%d" % i, [128, 512], F32)) for i in range(8)]
    ps = [Reg(ps_t[i], [Buf("ps%d" % i)], False) for i in range(8)]

    xs = Reg(xs_t, [[Buf("xs%d" % k)] for k in range(KD)], True)
    hh = Reg(h_t, [[Buf("h%d" % k)] for k in range(KD)], True)
    ring = [Reg(ring_t[i], [Buf("ring%d" % i)], False) for i in range(c.nslot)]
    wmT = Reg(wmT_t, [Buf("wmT")], False)
    cvec = Reg(cvec_t, [Buf("cvec")], False)
    consts = Reg(consts_t, [Buf("consts")], False)
    xbh = Reg(xbh_t, [[Buf("xbh%d" % k)] for k in range(L * c.KB)], True)
    zh = Reg(zh_t, [[Buf("zh%d" % k)] for k in range(L * c.KC)], True)
    rstd = Reg(rstd_t, [Buf("rstd")], False)
    stat = Reg(stat_t, [Buf("stat")], False)
    statA = Reg(stat_t, [Buf("statA")], False)

    class Arena:
        def __init__(self):
            self.off = 0

        def reset(self):
            self.off = 0

        def alloc(self, shape, dt, per_chunk=False):
            esz = 4 if dt == F32 else 2
            n = 1
            for s_ in shape:
                n *= s_
            nbytes = n * esz
            off = (self.off + 31) // 32 * 32
            assert off + nbytes <= ARENA, ("arena overflow", off, nbytes, ARENA)
            self.off = off + nbytes
            ap = arena_t[:, off // 2:(off + nbytes) // 2]
            if dt == F32:
                ap = ap.bitcast(F32)
            if len(shape) == 2:
                ap = ap.rearrange("p (a b) -> p a b", b=shape[1])
            elif len(shape) == 3:
                ap = ap.rearrange("p (a b c) -> p a b c", b=shape[1], c=shape[2])
            if per_chunk:
                cb = nbytes // shape[0]
                bufs = []
                for i in range(shape[0]):
                    lo, hi = off + i * cb, off + (i + 1) * cb
                    bufs.append(gran[lo // GRAN:(hi - 1) // GRAN + 1])
                return Reg(ap, bufs, True)
            return Reg(ap, gran[off // GRAN:(off + nbytes - 1) // GRAN + 1], False)

    AR = Arena()

    ident = consts_t[:, 0, :]
    cmask = consts_t[:, 1, :]
    ones = consts_t[:, 2, :]

    job_ctr = [0]

    def next_half():
        hsel = job_ctr[0] % 2
        job_ctr[0] += 1
        return [ps[hsel * 4 + i] for i in range(4)]

    ring_i = [0]

    cur = {"ti": 0, "l": 0, "seq": 0, "C0": 0}

    def wfetch(src_ap, nk, ncols):
        slot_i = ring_i[0] % c.nslot
        ring_i[0] += 1
        slot = ring[slot_i]
        dst = slot.ap[:, 0:nk, 0:ncols]
        l_, seq, ti_ = cur["l"], cur["seq"], cur["ti"]
        cur["seq"] += 1
        assert seq < NSEQ, seq
        cache = wsc_parts[l_][seq // TPP][seq % TPP].rearrange("p (k c) -> p k c", c=512)[:, 0:nk, 0:ncols]
        wbt = l_ + (seq % 2)
        if wbt >= len(c.tiles) - 1 or c.nocache:
            wbt = 1 << 30
        if ti_ > wbt:
            S.dma("pool", lambda e, dst=dst, src=cache: e.dma_start(out=dst, in_=src),
                  "d_w%d" % slot_i, writes=slot.b())
        else:
            S.dma("pool", lambda e, dst=dst, src=src_ap: e.dma_start(out=dst, in_=src),
                  "d_w%d" % slot_i, writes=slot.b())
            if ti_ == wbt:
                S.dma("sp", lambda e, dst=dst, cache=cache: e.dma_start(out=cache, in_=dst),
                      "d_b%d" % slot_i, reads=slot.b())
        return slot

    def wsrc(w_ap2d, r0, nk, c0, ncols):
        return w_ap2d[r0:r0 + nk * 128, c0:c0 + ncols].rearrange("(k p) c -> p k c", p=128)

    def fm_job(w2d, c0, ncols, rhs_reg, nkchunks, T, row0=0):
        nm = ncols // 128
        banks = next_half()
        nkt = (nkchunks + 3) // 4
        roff = cur["C0"] if rhs_reg is hh else 0
        for kt in range(nkt):
            nk = min(4, nkchunks - kt * 4)
            slot = wfetch(wsrc(w2d, row0 + kt * 512, nk, c0, ncols), nk, ncols)

            def fn(e, slot=slot, kt=kt, nk=nk):
                ins = None
                for m in range(nm):
                    for k in range(nk):
                        kk = kt * 4 + k
                        ins = e.matmul(banks[m].ap[:, 0:T], slot.ap[:, k, m * 128:(m + 1) * 128],
                                       rhs_reg.ap[:, kk, roff:roff + T],
                                       start=(kk == 0), stop=(kk == nkchunks - 1))
                return ins
            S.op("pe", fn, reads=slot.b() + rhs_reg.b(kt * 4, kt * 4 + nk),
                 writes=[bk.bufs[0] for bk in banks[:nm]])
        return banks[:nm]

    def act(fn, reads, writes):
        S.op("act", fn, reads, writes)

    def dve(fn, reads, writes):
        S.op("dve", fn, reads, writes)

    S.dma("sp", lambda e: e.dma_start(out=consts_t[:, :, :], in_=consts_d[:, :, :]), msem(), writes=consts.b())
    S.dma("sp", lambda e: e.dma_start(out=cvec_t[:, :], in_=cvec_d[:, :]), msem(), writes=cvec.b())
    dve(lambda e: e.memset(xbh_t[:, :, :], 0.0), [], xbh.b())
    dve(lambda e: e.memset(zh_t[:, :, :], 0.0), [], zh.b())
    AR.reset()
    wst = AR.alloc([L * c.AG, 128], F32)
    S.dma("sp", lambda e: e.dma_start(out=wst.ap, in_=wsT_d.rearrange("l g s t -> s (l g) t")), msem(),
          writes=wst.b())
    for lg in range(L * c.AG):
        dve(lambda e, lg=lg: e.tensor_tensor(out=wmT_t[:, lg, :], in0=wst.ap[:, lg, :], in1=cmask, op=ALU.mult),
            wst.b() + consts.b(), wmT.b())

    def load_x(tok0, T):
        nt = T // 128
        AR.reset()
        stg = [AR.alloc([1024], F32), AR.alloc([1024], F32)]
        nslab = D // 1024
        i = 0
        for tc in range(nt):
            for s_ in range(nslab):
                st = stg[i % 2]
                semn = "d_x%d" % (i % 2)
                S.dma("sp", lambda e, st=st, tc=tc, s_=s_: e.dma_start(
                    out=st.ap, in_=x_d[tok0 + tc * 128: tok0 + (tc + 1) * 128, s_ * 1024:(s_ + 1) * 1024]),
                    semn, writes=st.b())
                banks = next_half()
                for b2 in range(2):
                    def fn(e, st=st, b2=b2, banks=banks):
                        ins = None
                        for q in range(4):
                            ins = e.transpose(banks[b2].ap[:, q * 128:(q + 1) * 128],
                                              st.ap[:, (b2 * 4 + q) * 128:(b2 * 4 + q + 1) * 128], ident)
                        return ins
                    S.op("pe", fn, reads=st.b() + consts.b(), writes=banks[b2].b())
                    k0 = s_ * 8 + b2 * 4
                    src = banks[b2].ap[:, :].rearrange("p (a b) -> p a b", b=128)
                    dst = xs_t[:, k0:k0 + 4, tc * 128:(tc + 1) * 128]
                    if b2 == 0:
                        act(lambda e, dst=dst, src=src: e.activation(out=dst, in_=src, func=AF.Copy),
                            banks[b2].b(), xs.b(k0, k0 + 4))
                    else:
                        dve(lambda e, dst=dst, src=src: e.tensor_copy(out=dst, in_=src),
                            banks[b2].b(), xs.b(k0, k0 + 4))
                i += 1

    def rms_stats(T):
        sq = [AR.alloc([T], F32) for _ in range(8)]
        bank = next_half()[0]
        NG = KD // 4
        for g4 in range(NG):
            qs = sq[(g4 % 2) * 4:(g4 % 2) * 4 + 4]
            for i in range(4):
                k = g4 * 4 + i
                act(lambda e, q=qs[i], k=k: e.activation(out=q.ap, in_=xs_t[:, k, 0:T], func=AF.Square),
                    xs.b(k), qs[i].b())
            dve(lambda e, a=qs[0], b=qs[1]: e.tensor_tensor(out=a.ap, in0=a.ap, in1=b.ap, op=ALU.add),
                qs[0].b() + qs[1].b(), qs[0].b())
            dve(lambda e, a=qs[2], b=qs[3]: e.tensor_tensor(out=a.ap, in0=a.ap, in1=b.ap, op=ALU.add),
                qs[2].b() + qs[3].b(), qs[2].b())
            dve(lambda e, a=qs[0], b=qs[2]: e.tensor_tensor(out=a.ap, in0=a.ap, in1=b.ap, op=ALU.add),
                qs[0].b() + qs[2].b(), qs[0].b())
            S.op("pe", lambda e, q=qs[0], g4=g4: e.matmul(bank.ap[:, 0:T], ones, q.ap, start=(g4 == 0), stop=(g4 == NG - 1)),
                 reads=qs[0].b() + consts.b(), writes=bank.b())
        act(lambda e: e.activation(out=rstd_t[:, 0:T], in_=bank.ap[:, 0:T], func=AF.Sqrt, scale=1.0 / D, bias=EPS),
            bank.b(), rstd.b())
        dve(lambda e: e.reciprocal(out=rstd_t[:, 0:T], in_=rstd_t[:, 0:T]), rstd.b(), rstd.b())

    def rmsnorm_to_h(T, gcol0):
        rms_stats(T)
        for k in range(KD):
            dve(lambda e, k=k: e.scalar_tensor_tensor(out=h_t[:, k, 0:T], in0=xs_t[:, k, 0:T],
                                                      scalar=cvec_t[:, gcol0 + k:gcol0 + k + 1],
                                                      in1=rstd_t[:, 0:T], op0=ALU.mult, op1=ALU.mult),
                xs.b(k) + cvec.b() + rstd.b(), hh.b(k))

    def branch_a(l, T):
        nt = T // 128
        NFG = c.DA // 512
        win = w_in_d[l]
        C0 = cur["C0"]
        Tj = T - C0
        AR.reset()
        y_a = AR.alloc([c.KA, Tj], BF16, per_chunk=True)
        mark = AR.off
        v = AR.alloc([nt, c.DA], BF16, per_chunk=True)
        tmp = AR.alloc([c.DA], F32)
        lng = AR.alloc([c.DA], F32)
        lnb = AR.alloc([c.DA], F32)
        bsb = AR.alloc([c.AG, 128], F32)
        u_sb = AR.alloc([4, Tj], F32, per_chunk=True)
        tmp2s = [AR.alloc([T], F32), AR.alloc([T], F32)]
        S.dma("sp", lambda e: e.dma_start(out=lng.ap, in_=lng_d[l].partition_broadcast(128)), msem(), writes=lng.b())
        S.dma("sp", lambda e: e.dma_start(out=lnb.ap, in_=lnb_d[l].partition_broadcast(128)), msem(), writes=lnb.b())
        S.dma("sp", lambda e: e.dma_start(out=bsb.ap.rearrange("p a b -> p (a b)"),
                                          in_=bsp_d[l].partition_broadcast(128)), msem(), writes=bsb.b())
        if c.sub < 1:
            return y_a, mark
        for fg in range(NFG):
            banks = next_half()
            c0 = c.V0 + fg * 512
            for kt in range(KD // 4):
                slot = wfetch(wsrc(win, kt * 512, 4, c0, 512), 4, 512)

                def fn(e, slot=slot, kt=kt, banks=banks):
                    ins = None
                    for tc in range(nt):
                        for k in range(4):
                            kk = kt * 4 + k
                            ins = e.matmul(banks[tc].ap[:, :], h_t[:, kk, tc * 128:(tc + 1) * 128], slot.ap[:, k, :],
                                           start=(kk == 0), stop=(kk == KD - 1))
                    return ins
                S.op("pe", fn, reads=slot.b() + hh.b(kt * 4, kt * 4 + 4), writes=[bk.bufs[0] for bk in banks[:nt]])
            if c.sub < 2:
                continue
            for tc in range(nt):
                col = tc * NFG + fg
                vdst = v.ap[:, tc, fg * 512:(fg + 1) * 512]
                act(lambda e, tc=tc, vdst=vdst, banks=banks: e.activation(
                    out=vdst, in_=banks[tc].ap[:, :], func=AF.Gelu_apprx_tanh), banks[tc].b(), v.b(tc))
                dve(lambda e, vdst=vdst, col=col: e.tensor_reduce(out=stat_t[:, col:col + 1], in_=vdst,
                                                                  axis=AX.X, op=ALU.add), v.b(tc), statA.b())
                dve(lambda e, vdst=vdst, col=col: e.tensor_tensor(out=tmp.ap[:, 0:512], in0=vdst, in1=vdst,
                                                                  op=ALU.mult), v.b(tc), tmp.b())
                dve(lambda e, col=col: e.tensor_reduce(out=stat_t[:, 16 + col:17 + col], in_=tmp.ap[:, 0:512],
                                                       axis=AX.X, op=ALU.add), tmp.b(), stat.b())
        if c.sub < 3:
            return y_a, mark
        NS = nt * NFG
        dve(lambda e: e.tensor_reduce(out=stat_t[:, 32:32 + nt],
                                      in_=stat_t[:, 0:NS].rearrange("p (a b) -> p a b", b=NFG),
                                      axis=AX.X, op=ALU.add), statA.b(), stat.b())
        dve(lambda e: e.tensor_reduce(out=stat_t[:, 36:36 + nt],
                                      in_=stat_t[:, 16:16 + NS].rearrange("p (a b) -> p a b", b=NFG),
                                      axis=AX.X, op=ALU.add), stat.b(), stat.b())
        dve(lambda e: e.tensor_scalar(out=stat_t[:, 40:40 + nt], in0=stat_t[:, 32:32 + nt], scalar1=1.0 / c.DA,
                                      scalar2=None, op0=ALU.mult), stat.b(), stat.b())
        dve(lambda e: e.tensor_tensor(out=stat_t[:, 44:44 + nt], in0=stat_t[:, 40:40 + nt], in1=stat_t[:, 40:40 + nt],
                                      op=ALU.mult), stat.b(), stat.b())
        dve(lambda e: e.scalar_tensor_tensor(out=stat_t[:, 44:44 + nt], in0=stat_t[:, 36:36 + nt], scalar=1.0 / c.DA,
                                             in1=stat_t[:, 44:44 + nt], op0=ALU.mult, op1=ALU.subtract),
            stat.b(), stat.b())
        act(lambda e: e.activation(out=stat_t[:, 48:48 + nt], in_=stat_t[:, 44:44 + nt], func=AF.Sqrt, scale=1.0, bias=EPS),
            stat.b(), stat.b())
        dve(lambda e: e.reciprocal(out=stat_t[:, 48:48 + nt], in_=stat_t[:, 48:48 + nt]), stat.b(), stat.b())
        if c.sub < 4:
            return y_a, mark
        for tc in range(nt):
            dve(lambda e, tc=tc: e.tensor_scalar(out=tmp.ap, in0=v.ap[:, tc, :], scalar1=stat_t[:, 40 + tc:41 + tc],
                                                 scalar2=stat_t[:, 48 + tc:49 + tc], op0=ALU.subtract, op1=ALU.mult),
                v.b(tc) + stat.b(), tmp.b())
            if c.sub < 6:
                continue
            dve(lambda e: e.tensor_tensor(out=tmp.ap, in0=tmp.ap, in1=lng.ap, op=ALU.mult), tmp.b() + lng.b(), tmp.b())
            if c.sub < 7:
                continue
            dve(lambda e, tc=tc: e.tensor_tensor(out=v.ap[:, tc, :], in0=tmp.ap, in1=lnb.ap, op=ALU.add),
                tmp.b() + lnb.b(), v.b(tc))
        if c.sub < 8:
            return y_a, mark
        for cgu in range(c.DA // 512):
            banks = fm_job(win, c.U0 + cgu * 512, 512, hh, KD, Tj)
            for m in range(4):
                act(lambda e, m=m, banks=banks: e.activation(out=u_sb.ap[:, m, :], in_=banks[m].ap[:, 0:Tj],
                                                             func=AF.Gelu_apprx_tanh), banks[m].b(), u_sb.b(m))
            mb = next_half()
            for m in range(4):
                j = cgu * 4 + m
                g = j // 2

                def fn(e, m=m, j=j, g=g, mb=mb):
                    ins = None
                    for tc in range(nt):
                        ins = e.matmul(mb[m].ap[:, tc * 128:(tc + 1) * 128], v.ap[:, tc, j * 128:(j + 1) * 128],
                                       wmT_t[:, l * c.AG + g, :], start=True, stop=True)
                    return ins
                S.op("pe", fn, reads=v.b() + wmT.b(), writes=mb[m].b())
                t2 = tmp2s[m % 2]
                dve(lambda e, m=m, g=g, mb=mb, t2=t2: e.tensor_tensor(
                    out=t2.ap.rearrange("p (a b) -> p a b", b=128),
                    in0=mb[m].ap[:, 0:T].rearrange("p (a b) -> p a b", b=128),
                    in1=bsb.ap[:, g:g + 1, :].to_broadcast([128, nt, 128]), op=ALU.add),
                    mb[m].b() + bsb.b(), t2.b())
                dve(lambda e, m=m, j=j, t2=t2: e.tensor_tensor(out=y_a.ap[:, j, :], in0=t2.ap[:, C0:T], in1=u_sb.ap[:, m, :],
                                                               op=ALU.mult), t2.b() + u_sb.b(m), y_a.b(j))
        return y_a, mark

    def branch_b(l, T, y_b, use_tab, halo_only):
        win = w_in_d[l]
        CB = c.CB
        W16 = T + 16
        xbuf = [AR.alloc([W16], F32) for _ in range(4)]
        pa = [AR.alloc([W16], F32) for _ in range(4)]
        pb = [AR.alloc([W16], F32) for _ in range(4)]
        pooled = [AR.alloc([CB, T], BF16, per_chunk=True) for _ in range(4)]
        tab = None
        if use_tab and not halo_only:
            tab = AR.alloc([4, T], F32)
            S.dma("sp", lambda e, C0=cur["C0"]: e.dma_start(out=tab.ap, in_=invc_d[:, :, C0:C0 + T]), msem(), writes=tab.b())
        pend = []
        for jj in range(c.DB // 512):
            while len(pend) > 1:
                pend.pop(0)()
            banks = fm_job(win, c.XB0 + jj * 512, 512, hh, KD, T)
            if pend:
                pend.pop(0)()
            for m in range(4):
                j = jj * 4 + m
                xb_ = xbuf[m]
                hidx = l * c.KB + j
                act(lambda e, xb_=xb_, hidx=hidx: e.activation(out=xb_.ap[:, 0:16], in_=xbh_t[:, hidx, :], func=AF.Copy),
                    xbh.b(hidx), xb_.b())
                act(lambda e, xb_=xb_, m=m, banks=banks: e.activation(out=xb_.ap[:, 16:16 + T], in_=banks[m].ap[:, 0:T],
                                                                    func=AF.Copy), banks[m].b(), xb_.b())
                act(lambda e, xb_=xb_, hidx=hidx: e.activation(out=xbh_t[:, hidx, :], in_=xb_.ap[:, T:T + 16], func=AF.Copy),
                    xb_.b(), xbh.b(hidx))
            if halo_only:
                continue
            srcs = [xbuf[m] for m in range(4)]
            los = [0, 0, 0, 0]
            gs = [(jj * 4 + m) // CB for m in range(4)]
            for stp in range(max(gs) + 1):
                for m in range(4):
                    if stp > gs[m]:
                        continue
                    sh = 1 << stp
                    dst = pa[m] if stp % 2 == 0 else pb[m]
                    src = srcs[m]
                    lo2 = los[m] + sh
                    dve(lambda e, src=src, dst=dst, lo2=lo2, sh=sh: e.tensor_tensor(
                        out=dst.ap[:, lo2:W16], in0=src.ap[:, lo2:W16], in1=src.ap[:, lo2 - sh:W16 - sh], op=ALU.add),
                        src.b(), dst.b())
                    srcs[m] = dst
                    los[m] = lo2
            if tab is not None:
                for m in range(4):
                    dve(lambda e, src=srcs[m], g=gs[m]: e.tensor_tensor(out=src.ap[:, 16:W16], in0=src.ap[:, 16:W16],
                                                                        in1=tab.ap[:, g, :], op=ALU.mult),
                        srcs[m].b() + tab.b(), srcs[m].b())
            for m in range(4):
                j = jj * 4 + m
                g = gs[m]
                src = srcs[m]
                xb_ = xbuf[m]
                pl = pooled[g % 4]
                wn = float(1 << (g + 1))
                if tab is None:
                    dve(lambda e, src=src, xb_=xb_, pl=pl, j=j, wn=wn: e.scalar_tensor_tensor(
                        out=pl.ap[:, j % CB, :], in0=src.ap[:, 16:W16], scalar=1.0 / wn, in1=xb_.ap[:, 16:W16],
                        op0=ALU.mult, op1=ALU.subtract), src.b() + xb_.b(), pl.b(j % CB))
                else:
                    dve(lambda e, src=src, xb_=xb_, pl=pl, j=j: e.tensor_tensor(
                        out=pl.ap[:, j % CB, :], in0=src.ap[:, 16:W16], in1=xb_.ap[:, 16:W16], op=ALU.subtract),
                        src.b() + xb_.b(), pl.b(j % CB))
                if (j + 1) % CB == 0:
                    def emit_pool(g=g, pl=pl):
                        pbanks = next_half()
                        slot = wfetch(wsrc(w_pool_d[l, g], 0, CB, 0, CB * 128), CB, CB * 128)

                        def fn(e, slot=slot, pl=pl, pbanks=pbanks):
                            ins = None
                            for mm in range(CB):
                                for k in range(CB):
                                    ins = e.matmul(pbanks[mm].ap[:, 0:T], slot.ap[:, k, mm * 128:(mm + 1) * 128],
                                                   pl.ap[:, k, :], start=(k == 0), stop=(k == CB - 1))
                            return ins
                        S.op("pe", fn, reads=slot.b() + pl.b(), writes=[bk.bufs[0] for bk in pbanks[:CB]])
                        for mm in range(CB):
                            jo = g * CB + mm
                            col = l * c.CV_L + c.CV_PSC + jo
                            act(lambda e, mm=mm, jo=jo, col=col, pbanks=pbanks: e.activation(
                                out=y_b.ap[:, jo, :], in_=pbanks[mm].ap[:, 0:T], func=AF.Copy,
                                scale=cvec_t[:, col:col + 1]), pbanks[mm].b() + cvec.b(), y_b.b(jo))
                    if os.environ.get('KNODEFER', '0') == '1':
                        emit_pool()
                    else:
                        pend.append(emit_pool)
        while len(pend) > 1:
            pend.pop(0)()
        return pend

    def branch_c(l, T, y_c, halo_only, pre_jobs=()):
        pre_jobs = list(pre_jobs)
        win = w_in_d[l]
        cc_sb = AR.alloc([4, T], F32, per_chunk=True)
        z = AR.alloc([4, T + 2], F32, per_chunk=True)
        acc = AR.alloc([4, T], F32, per_chunk=True)
        for jj in range(c.DC // 512):
            banks = fm_job(win, c.CC0 + jj * 512, 512, hh, KD, T)
            while pre_jobs:
                pre_jobs.pop(0)()
            for m in range(4):
                act(lambda e, m=m, banks=banks: e.activation(out=cc_sb.ap[:, m, :], in_=banks[m].ap[:, 0:T], func=AF.Copy),
                    banks[m].b(), cc_sb.b(m))
            banks = fm_job(win, c.HC0 + jj * 512, 512, hh, KD, T)
            for m in range(4):
                j = jj * 4 + m
                hidx = l * c.KC + j
                act(lambda e, m=m, hidx=hidx: e.activation(out=z.ap[:, m, 0:2], in_=zh_t[:, hidx, :], func=AF.Copy),
                    zh.b(hidx), z.b(m))
                dve(lambda e, m=m, banks=banks: e.tensor_tensor(out=z.ap[:, m, 2:2 + T], in0=cc_sb.ap[:, m, :],
                                                                in1=banks[m].ap[:, 0:T], op=ALU.mult),
                    cc_sb.b(m) + banks[m].b(), z.b(m))
                act(lambda e, m=m, hidx=hidx: e.activation(out=zh_t[:, hidx, :], in_=z.ap[:, m, T:T + 2], func=AF.Copy),
                    z.b(m), zh.b(hidx))
                if halo_only:
                    continue
                cw = l * c.CV_L + c.CV_CONV
                dve(lambda e, m=m, j=j, cw=cw: e.tensor_scalar(
                    out=acc.ap[:, m, :], in0=z.ap[:, m, 2:2 + T], scalar1=cvec_t[:, cw + 2 * c.KC + j:cw + 2 * c.KC + j + 1],
                    scalar2=None, op0=ALU.mult), z.b(m) + cvec.b(), acc.b(m))
                dve(lambda e, m=m, j=j, cw=cw: e.scalar_tensor_tensor(
                    out=acc.ap[:, m, :], in0=z.ap[:, m, 1:1 + T], scalar=cvec_t[:, cw + c.KC + j:cw + c.KC + j + 1],
                    in1=acc.ap[:, m, :], op0=ALU.mult, op1=ALU.add), z.b(m) + cvec.b() + acc.b(m), acc.b(m))
                dve(lambda e, m=m, j=j, cw=cw: e.scalar_tensor_tensor(
                    out=acc.ap[:, m, :], in0=z.ap[:, m, 0:T], scalar=cvec_t[:, cw + j:cw + j + 1],
                    in1=acc.ap[:, m, :], op0=ALU.mult, op1=ALU.add), z.b(m) + cvec.b() + acc.b(m), acc.b(m))
            if halo_only:
                continue
            banks = fm_job(win, c.BC0 + jj * 512, 512, hh, KD, T)
            for m in range(4):
                j = jj * 4 + m
                dve(lambda e, m=m, j=j, banks=banks: e.tensor_tensor(out=y_c.ap[:, j, :], in0=banks[m].ap[:, 0:T],
                                                                    in1=acc.ap[:, m, :], op=ALU.mult),
                    banks[m].b() + acc.b(m), y_c.b(j))

    def merge_and_out(l, T, ys):
        win = w_in_d[l]
        sg = AR.alloc([4, T], F32, per_chunk=True)
        acc = AR.alloc([4, T], F32, per_chunk=True)
        mg = [AR.alloc([4, T], BF16, per_chunk=True), AR.alloc([4, T], BF16, per_chunk=True)]
        pend_out = []

        def emit_out(cg):
            for n in range(D // 512):
                banks = fm_job(w_out_d[l], n * 512, 512, mg[cg % 2], 4, T, row0=cg * 512)
                for m in range(4):
                    kx = n * 4 + m
                    dve(lambda e, m=m, kx=kx, banks=banks, C0=cur["C0"]: e.tensor_tensor(
                        out=xs_t[:, kx, C0:C0 + T], in0=xs_t[:, kx, C0:C0 + T], in1=banks[m].ap[:, 0:T], op=ALU.add),
                        xs.b(kx) + banks[m].b(), xs.b(kx))

        for cg in range(D // 512):
            for br in range(3):
                banks = fm_job(win, c.G0 + br * D + cg * 512, 512, hh, KD, T)
                for m in range(4):
                    act(lambda e, m=m, banks=banks: e.activation(out=sg.ap[:, m, :], in_=banks[m].ap[:, 0:T],
                                                                 func=AF.Sigmoid), banks[m].b(), sg.b(m))
                if br == 0 and pend_out:
                    emit_out(pend_out.pop(0))
                banks = fm_job(w_br_d[br][l], cg * 512, 512, ys[br], c.KA, T)
                for m in range(4):
                    if br == 0:
                        dve(lambda e, m=m, banks=banks: e.tensor_tensor(out=acc.ap[:, m, :], in0=sg.ap[:, m, :],
                                                                        in1=banks[m].ap[:, 0:T], op=ALU.mult),
                            sg.b(m) + banks[m].b(), acc.b(m))
                    else:
                        dve(lambda e, m=m, banks=banks: e.tensor_tensor(out=sg.ap[:, m, :], in0=sg.ap[:, m, :],
                                                                        in1=banks[m].ap[:, 0:T], op=ALU.mult),
                            sg.b(m) + banks[m].b(), sg.b(m))
                        if br == 1:
                            dve(lambda e, m=m: e.tensor_tensor(out=acc.ap[:, m, :], in0=acc.ap[:, m, :],
                                                               in1=sg.ap[:, m, :], op=ALU.add),
                                sg.b(m) + acc.b(m), acc.b(m))
                        else:
                            dve(lambda e, m=m, mgc=mg[cg % 2]: e.tensor_tensor(out=mgc.ap[:, m, :], in0=acc.ap[:, m, :],
                                                                              in1=sg.ap[:, m, :], op=ALU.add),
                                sg.b(m) + acc.b(m), mg[cg % 2].b(m))
            pend_out.append(cg)
        while pend_out:
            emit_out(pend_out.pop(0))

    def ffn(l, T):
        AR.reset()
        halves = [(0, (c.KF + 1) // 2), ((c.KF + 1) // 2, c.KF)]
        ff = AR.alloc([(c.KF + 1) // 2, T], BF16, per_chunk=True)
        sgf = AR.alloc([4, T], F32, per_chunk=True)
        for (f0, f1) in halves:
            nf = f1 - f0
            fi = 0
            while fi < nf:
                nm = min(4, nf - fi)
                c0 = (f0 + fi) * 128
                banks = fm_job(w_gate_d[l], c0, nm * 128, hh, KD, T)
                for m in range(nm):
                    act(lambda e, m=m, banks=banks: e.activation(out=sgf.ap[:, m, :], in_=banks[m].ap[:, 0:T],
                                                                 func=AF.Silu), banks[m].b(), sgf.b(m))
                banks = fm_job(w_up_d[l], c0, nm * 128, hh, KD, T)
                for m in range(nm):
                    dve(lambda e, m=m, fi=fi, banks=banks: e.tensor_tensor(out=ff.ap[:, fi + m, :], in0=sgf.ap[:, m, :],
                                                                          in1=banks[m].ap[:, 0:T], op=ALU.mult),
                        sgf.b(m) + banks[m].b(), ff.b(fi + m))
                fi += nm
            for n in range(D // 512):
                banks = fm_job(w_down_d[l], n * 512, 512, ff, nf, T, row0=f0 * 128)
                for m in range(4):
                    kx = n * 4 + m
                    dve(lambda e, m=m, kx=kx, banks=banks, C0=cur["C0"]: e.tensor_tensor(
                        out=xs_t[:, kx, C0:C0 + T], in0=xs_t[:, kx, C0:C0 + T], in1=banks[m].ap[:, 0:T], op=ALU.add),
                        xs.b(kx) + banks[m].b(), xs.b(kx))

    def final_store(tok0_, T, skip):
        nt = T // 128
        tok_out0 = tok0_ - c.NPRE
        AR.reset()
        rms_stats(T)
        osl = AR.alloc([8, T], F32, per_chunk=True)
        ost = [AR.alloc([1024], F32), AR.alloc([1024], F32)]
        i = 0
        for s_ in range(D // 1024):
            for k in range(8):
                kk = s_ * 8 + k
                dve(lambda e, k=k, kk=kk: e.scalar_tensor_tensor(out=osl.ap[:, k, :], in0=xs_t[:, kk, 0:T],
                                                                 scalar=cvec_t[:, c.CV_FIN + kk:c.CV_FIN + kk + 1],
                                                                 in1=rstd_t[:, 0:T], op0=ALU.mult, op1=ALU.mult),
                    xs.b(kk) + cvec.b() + rstd.b(), osl.b(k))
            for tc in range(skip, nt):
                st = ost[i % 2]
                semn = "d_o%d" % (i % 2)
                banks = next_half()
                for b2 in range(2):
                    def fn(e, b2=b2, tc=tc, banks=banks):
                        ins = None
                        for q in range(4):
                            ins = e.transpose(banks[b2].ap[:, q * 128:(q + 1) * 128],
                                              osl.ap[:, b2 * 4 + q, tc * 128:(tc + 1) * 128], ident)
                        return ins
                    S.op("pe", fn, reads=osl.b(b2 * 4, b2 * 4 + 4) + consts.b(), writes=banks[b2].b())
                    if b2 == 0:
                        act(lambda e, st=st, banks=banks: e.activation(out=st.ap[:, 0:512], in_=banks[0].ap[:, :],
                                                                       func=AF.Copy), banks[0].b(), st.b())
                    else:
                        dve(lambda e, st=st, banks=banks: e.tensor_copy(out=st.ap[:, 512:1024], in_=banks[1].ap[:, :]),
                            banks[1].b(), st.b())
                S.dma("sp", lambda e, st=st, tc=tc, s_=s_: e.dma_start(
                    out=y_d[tok_out0 + tc * 128: tok_out0 + (tc + 1) * 128, s_ * 1024:(s_ + 1) * 1024], in_=st.ap),
                    semn, reads=st.b())
                i += 1

    STAGES = ("load", "norm", "A", "B", "C", "M", "ffn", None)

    def upto(name):
        return STAGES.index(c.stop) >= STAGES.index(name)

    def layer(l, T, first_tile):
        cv = l * c.CV_L
        AR.reset()
        if not upto("norm"):
            return
        rmsnorm_to_h(T, cv + c.CV_GMIX)
        if not upto("A"):
            return
        y_a, mark = branch_a(l, T)
        Tfull = T
        T = T - cur["C0"]
        AR.off = mark
        y_b = AR.alloc([c.KB, T], BF16, per_chunk=True)
        mark_b = AR.off
        if not upto("B"):
            return
        pend_b = branch_b(l, T, y_b, first_tile, False)
        AR.off = mark_b
        y_c = AR.alloc([c.KC, T], BF16, per_chunk=True)
        mark_c = AR.off
        if not upto("C"):
            for f_ in pend_b:
                f_()
            return
        branch_c(l, T, y_c, False, pend_b)
        AR.off = mark_c
        if not upto("M"):
            return
        merge_and_out(l, T, [y_a, y_b, y_c])
        AR.reset()
        if not upto("ffn"):
            return
        rmsnorm_to_h(Tfull, cv + c.CV_GFFN)
        ffn(l, T)

    tok0 = 0
    for ti, T in enumerate(c.tiles):
        cur["ti"] = ti
        cur["C0"] = c.C0 if ti == 0 else 0
        if ti >= 1:
            S.pending["pool"] = [("d_b%d" % i, S.cnt["d_b%d" % i]) for i in range(c.nslot) if ("d_b%d" % i) in S.cnt]
        load_x(tok0, T)
        for l in range(L):
            cur["l"] = l
            cur["seq"] = 0
            layer(l, T, ti == 0)
        final_store(tok0, T, 1 if ti == 0 else 0)
        tok0 += T

    final_waits = [(s_, S.cnt[s_]) for s_ in ("d_o0", "d_o1") if s_ in S.cnt]

    sem_ctx = {n: es.enter_context(nc.semaphore(n)) for n in sem_names}
    block = es.enter_context(nc.Block())

    def replay(engname, e):
        for waits, fn, sem, inc in S.ops[engname]:
            for (s_, v) in waits:
                e.wait_ge(sem_ctx[s_], v)
            ins = fn(e)
            ins.then_inc(sem_ctx[sem], inc)

    @block.tensor
    def _(e):
        replay("pe", e)

    @block.scalar
    def _(e):
        replay("act", e)

    @block.vector
    def _(e):
        replay("dve", e)

    @block.gpsimd
    def _(e):
        replay("pool", e)

    @block.sync
    def _(e):
        replay("sp", e)
        for (s_, v) in final_waits:
            e.wait_ge(sem_ctx[s_], v)

    es.close()
    return nc, S


def host_inputs(cfg, inp, core):
    c = cfg
    x = inp["x"].reshape(-1, c.D)
    per = c.NTOK
    lo = core * per - c.NPRE
    if lo < 0:
        xc = np.concatenate([np.zeros((c.NPRE, c.D), np.float32), x[0:per]], axis=0)
    else:
        xc = np.ascontiguousarray(x[lo:lo + c.NPRE + per])
    m = {"x": xc}
    return m


def shared_inputs(cfg, inp):
    c = cfg
    L = c.L
    sh = {}
    for k in ("w_in", "w_pool", "w_branch_a", "w_branch_b", "w_branch_c", "w_out",
              "w_ffn_gate", "w_ffn_up", "w_ffn_down", "ln_a_g", "ln_a_b"):
        sh[k] = np.ascontiguousarray(inp[k], dtype=np.float32)
    sh["wsT"] = np.ascontiguousarray(np.transpose(inp["w_spatial"], (0, 1, 3, 2)))
    sh["b_spatial"] = np.ascontiguousarray(inp["b_spatial"].reshape(L, -1))
    cv = np.zeros((128, c.NCV), np.float32)
    for l in range(L):
        b = l * c.CV_L
        cv[:, b + c.CV_GMIX:b + c.CV_GMIX + c.KD] = inp["norm_mix_g"][l].reshape(c.KD, 128).T
        cv[:, b + c.CV_GFFN:b + c.CV_GFFN + c.KD] = inp["norm_ffn_g"][l].reshape(c.KD, 128).T
        cv[:, b + c.CV_PSC:b + c.CV_PSC + c.KB] = inp["pool_scale"][l].reshape(c.KB, 128).T
        for j in range(3):
            cv[:, b + c.CV_CONV + j * c.KC:b + c.CV_CONV + (j + 1) * c.KC] = inp["conv_w"][l, j].reshape(c.KC, 128).T
    cv[:, c.CV_FIN:c.CV_FIN + c.KD] = inp["final_norm_g"].reshape(c.KD, 128).T
    sh["cvec"] = cv
    cs = np.zeros((128, 3, 128), np.float32)
    cs[:, 0, :] = np.eye(128, dtype=np.float32)
    cs[:, 1, :] = np.triu(np.ones((128, 128), np.float32))
    cs[:, 2, :] = 1.0
    sh["consts"] = cs
    return sh


def invc_table(cfg, core):
    c = cfg
    T0 = c.tiles[0]
    pos = core * c.NTOK + np.arange(1 - c.NPRE, T0 - c.NPRE + 1, dtype=np.float32)
    pos = np.maximum(pos, 1.0)
    tab = np.zeros((128, 4, T0), np.float32)
    for g in range(4):
        win = float(2 ** (g + 1))
        tab[:, g, :] = (np.float32(1.0) / np.minimum(pos, win))[None, :]
    return tab


_CACHE = {}


def run(cfg, inp):
    key = (cfg.D, cfg.DFF, cfg.tiles, cfg.ncores)
    if key not in _CACHE:
        _CACHE[key] = build(cfg)[0]
    nc = _CACHE[key]
    sh = shared_inputs(cfg, inp)
    in_maps = []
    for core in range(cfg.ncores):
        m = dict(sh)
        m.update(host_inputs(cfg, inp, core))
        m["invc"] = invc_table(cfg, core)
        in_maps.append(m)
    res = run_bass_kernel_spmd(nc, in_maps, core_ids=list(range(cfg.ncores)))
    out = np.concatenate([r["y"] for r in res.results], axis=0)
    return out


def kernel(**inputs):
    cfg = Cfg()
    inp = {k: np.asarray(v) for k, v in inputs.items()}
    out = run(cfg, inp)
    return out.reshape(1, cfg.ncores * cfg.NTOK, cfg.D).astype(np.float32, copy=False)
```

```python
import os
import numpy as np
import concourse.bass as bass
import concourse.mybir as mybir
from concourse.bass_utils import run_bass_kernel_spmd

F32 = mybir.dt.float32
BF16 = mybir.dt.bfloat16
AF = mybir.ActivationFunctionType
ALU = mybir.AluOpType
AX = mybir.AxisListType

EPS = 1e-6
ENGS = ("pe", "act", "dve", "pool", "sp")


class Cfg:
    def __init__(self, D=4096, DFF=11008, L=2, tiles=(512, 512, 384, 384, 384), ncores=8, nslot=5, stop=None):
        self.stop = stop
        import os
        self.sub = int(os.environ.get('KSUB', '99'))
        self.nocache = os.environ.get('KNOCACHE', '0') == '1'
        self.D, self.DFF, self.L = D, DFF, L
        self.tiles = tuple(tiles)
        self.ncores = ncores
        self.nslot = nslot
        self.KD = D // 128
        self.DA = D // 2
        self.KA = self.DA // 128
        self.AG = self.DA // 256
        self.DB = D // 2
        self.KB = self.DB // 128
        self.CB = self.KB // 4
        self.DC = D // 2
        self.KC = self.DC // 128
        self.NIN = 2 * self.DA + self.DB + 3 * self.DC + 3 * D
        self.U0 = 0
        self.V0 = self.DA
        self.XB0 = 2 * self.DA
        self.BC0 = self.XB0 + self.DB
        self.CC0 = self.BC0 + self.DC
        self.HC0 = self.CC0 + self.DC
        self.G0 = self.HC0 + self.DC
        self.KF = DFF // 128
        self.TMAX = max(tiles)
        self.NPRE = 128
        self.C0 = 96
        self.NTOK = sum(tiles) - 128
        self.CV_GMIX = 0
        self.CV_GFFN = self.KD
        self.CV_PSC = 2 * self.KD
        self.CV_CONV = 2 * self.KD + self.KB
        self.CV_L = 2 * self.KD + self.KB + 3 * self.KC
        self.CV_FIN = self.L * self.CV_L
        self.NCV = self.CV_FIN + self.KD


class Buf:
    __slots__ = ("name", "lw", "rd")

    def __init__(self, name):
        self.name = name
        self.lw = None
        self.rd = {}


class Sched:
    def __init__(self):
        self.ops = {e: [] for e in ENGS}
        self.cnt = {}
        self.known = {e: {} for e in ENGS}
        self.nwait = 0
        self.pending = {}

    def _emit(self, eng, fn, reads, writes, sem, inc, ident, selfwait=False):
        need = {}

        def add(s, v):
            if self.known[eng].get(s, 0) >= v:
                return
            if need.get(s, 0) < v:
                need[s] = v

        if selfwait and self.cnt.get(sem, 0) > 0:
            add(sem, self.cnt[sem])
        for (ps_, pv_) in self.pending.pop(eng, ()):
            add(ps_, pv_)
        for b in reads:
            if b.lw is not None:
                add(b.lw[0], b.lw[1])
        for b in writes:
            if b.lw is not None and (ident != "pe" or b.lw[2] != ident):
                add(b.lw[0], b.lw[1])
            for s, (v, e) in b.rd.items():
                if ident != "pe" or e != ident:
                    add(s, v)
        for s, v in need.items():
            self.known[eng][s] = v
        self.nwait += len(need)
        val = self.cnt.get(sem, 0) + inc
        self.cnt[sem] = val
        for b in writes:
            b.lw = (sem, val, ident)
            b.rd = {}
        for b in reads:
            b.rd[sem] = (val, ident)
        self.ops[eng].append((tuple(need.items()), fn, sem, inc))
        return val

    def op(self, eng, fn, reads=(), writes=()):
        return self._emit(eng, fn, reads, writes, "c_" + eng, 1, eng)

    def dma(self, queue, fn, sem, reads=(), writes=()):
        return self._emit(queue, fn, reads, writes, sem, 16, None, selfwait=True)


class Reg:
    def __init__(self, ap, bufs, per_chunk):
        self.ap = ap
        self.bufs = bufs
        self.per_chunk = per_chunk

    def b(self, i=None, j=None):
        if not self.per_chunk or i is None:
            if not self.per_chunk:
                return list(self.bufs)
            out = []
            for x in self.bufs:
                for y in x:
                    if y not in out:
                        out.append(y)
            return out
        if j is None:
            return list(self.bufs[i])
        out = []
        for x in self.bufs[i:j]:
            for y in x:
                if y not in out:
                    out.append(y)
        return out


GRAN = 2048


def build(cfg):
    c = cfg
    D, KD, L = c.D, c.KD, c.L
    TM = c.TMAX
    nc = bass.Bass("TRN2", target_bir_lowering=False)

    def din(name, shape):
        return nc.dram_tensor(name, list(shape), F32, kind="ExternalInput").ap()

    x_d = din("x", [c.NPRE + c.NTOK, D])
    w_in_d = din("w_in", [L, D, c.NIN])
    w_pool_d = din("w_pool", [L, 4, c.DB // 4, c.DB // 4])
    w_br_d = [din("w_branch_a", [L, c.DA, D]), din("w_branch_b", [L, c.DB, D]), din("w_branch_c", [L, c.DC, D])]
    w_out_d = din("w_out", [L, D, D])
    w_gate_d = din("w_ffn_gate", [L, D, c.DFF])
    w_up_d = din("w_ffn_up", [L, D, c.DFF])
    w_down_d = din("w_ffn_down", [L, c.DFF, D])
    wsT_d = din("wsT", [L, c.AG, 128, 128])
    cvec_d = din("cvec", [128, c.NCV])
    lng_d = din("ln_a_g", [L, c.DA])
    lnb_d = din("ln_a_b", [L, c.DA])
    bsp_d = din("b_spatial", [L, c.AG * 128])
    consts_d = din("consts", [128, 3, 128])
    invc_d = din("invc", [128, 4, c.tiles[0]])
    y_d = nc.dram_tensor("y", [c.NTOK, D], F32, kind="ExternalOutput").ap()
    NSEQ = (c.NIN // 512) * (KD // 4) + 4 + 3 * (D // 512) * (c.KA // 4) + (D // 512) * (D // 512) \
        + 2 * ((c.KF + 3) // 4 + 2) * (KD // 4) + 2 * (D // 512) * (((c.KF + 1) // 2 + 3) // 4) + 8
    TPP = 448
    NPART = (NSEQ + TPP - 1) // TPP
    wsc_parts = [[nc.dram_tensor("wsc_%d_%d" % (l_, p_), [min(TPP, NSEQ - p_ * TPP), 128, 2048], BF16,
                                 kind="Internal").ap() for p_ in range(NPART)] for l_ in range(L)]

    S = Sched()
    NMISC = 8
    sem_names = ["c_pe", "c_act", "c_dve", "c_pool", "d_x0", "d_x1", "d_x2", "d_x3", "d_o0", "d_o1"] + \
                ["d_m%d" % i for i in range(NMISC)] + ["d_w%d" % i for i in range(c.nslot)] + \
                ["d_b%d" % i for i in range(c.nslot)]
    misc_i = [0]

    def msem():
        misc_i[0] += 1
        return "d_m%d" % (misc_i[0] % NMISC)

    import contextlib
    es = contextlib.ExitStack()

    def sb(name, shape, dt):
        return es.enter_context(nc.sbuf_tensor(name, list(shape), dt))

    xs_t = sb("xs", [128, KD, TM], F32)
    h_t = sb("h", [128, KD, TM], BF16)
    ring_t = [sb("ring%d" % i, [128, 4, 512], BF16) for i in range(c.nslot)]
    wmT_t = sb("wmT", [128, L * c.AG, 128], BF16)
    cvec_t = sb("cvec_sb", [128, c.NCV], F32)
    consts_t = sb("consts_sb", [128, 3, 128], F32)
    xbh_t = sb("xbhalo", [128, L * c.KB, 16], F32)
    zh_t = sb("zhalo", [128, L * c.KC, 2], F32)
    rstd_t = sb("rstd", [128, TM], F32)
    stat_t = sb("stat", [128, 64], F32)

    ARENA = max(3 * c.KA * TM * 2 + 3 * 4 * TM * 4 + 2048, 48 * 1024)
    ARENA = (ARENA + GRAN - 1) // GRAN * GRAN
    arena_t = sb("arena", [128, ARENA // 2], BF16)
    gran = [Buf("g%d" % i) for i in range(ARENA // GRAN)]

    ps_t = [es.enter_context(nc.psum_tensor("ps%d" % i, [128, 512], F32)) for i in range(8)]
    ps = [Reg(ps_t[i], [Buf("ps%d" % i)], False) for i in range(8)]

    xs = Reg(xs_t, [[Buf("xs%d" % k)] for k in range(KD)], True)
    hh = Reg(h_t, [[Buf("h%d" % k)] for k in range(KD)], True)
    ring = [Reg(ring_t[i], [Buf("ring%d" % i)], False) for i in range(c.nslot)]
    wmT = Reg(wmT_t, [Buf("wmT")], False)
    cvec = Reg(cvec_t, [Buf("cvec")], False)
    consts = Reg(consts_t, [Buf("consts")], False)
    xbh = Reg(xbh_t, [[Buf("xbh%d" % k)] for k in range(L * c.KB)], True)
    zh = Reg(zh_t, [[Buf("zh%d" % k)] for k in range(L * c.KC)], True)
    rstd = Reg(rstd_t, [Buf("rstd")], False)
    stat = Reg(stat_t, [Buf("stat")], False)
    statA = Reg(stat_t, [Buf("statA")], False)

    class Arena:
        def __init__(self):
            self.off = 0

        def reset(self):
            self.off = 0

        def alloc(self, shape, dt, per_chunk=False):
            esz = 4 if dt == F32 else 2
            n = 1
            for s_ in shape:
                n *= s_
            nbytes = n * esz
            off = (self.off + 31) // 32 * 32
            assert off + nbytes <= ARENA, ("arena overflow", off, nbytes, ARENA)
            self.off = off + nbytes
            ap = arena_t[:, off // 2:(off + nbytes) // 2]
            if dt == F32:
                ap = ap.bitcast(F32)
            if len(shape) == 2:
                ap = ap.rearrange("p (a b) -> p a b", b=shape[1])
            elif len(shape) == 3:
                ap = ap.rearrange("p (a b c) -> p a b c", b=shape[1], c=shape[2])
            if per_chunk:
                cb = nbytes // shape[0]
                bufs = []
                for i in range(shape[0]):
                    lo, hi = off + i * cb, off + (i + 1) * cb
                    bufs.append(gran[lo // GRAN:(hi - 1) // GRAN + 1])
                return Reg(ap, bufs, True)
            return Reg(ap, gran[off // GRAN:(off + nbytes - 1) // GRAN + 1], False)

    AR = Arena()

    ident = consts_t[:, 0, :]
    cmask = consts_t[:, 1, :]
    ones = consts_t[:, 2, :]

    job_ctr = [0]

    def next_half():
        hsel = job_ctr[0] % 2
        job_ctr[0] += 1
        return [ps[hsel * 4 + i] for i in range(4)]

    ring_i = [0]

    cur = {"ti": 0, "l": 0, "seq": 0, "C0": 0}

    def wfetch(src_ap, nk, ncols):
        slot_i = ring_i[0] % c.nslot
        ring_i[0] += 1
        slot = ring[slot_i]
        dst = slot.ap[:, 0:nk, 0:ncols]
        l_, seq, ti_ = cur["l"], cur["seq"], cur["ti"]
        cur["seq"] += 1
        assert seq < NSEQ, seq
        cache = wsc_parts[l_][seq // TPP][seq % TPP].rearrange("p (k c) -> p k c", c=512)[:, 0:nk, 0:ncols]
        wbt = l_ + (seq % 2)
        if wbt >= len(c.tiles) - 1 or c.nocache:
            wbt = 1 << 30
        if ti_ > wbt:
            S.dma("pool", lambda e, dst=dst, src=cache: e.dma_start(out=dst, in_=src),
                  "d_w%d" % slot_i, writes=slot.b())
        else:
            S.dma("pool", lambda e, dst=dst, src=src_ap: e.dma_start(out=dst, in_=src),
                  "d_w%d" % slot_i, writes=slot.b())
            if ti_ == wbt:
                S.dma("sp", lambda e, dst=dst, cache=cache: e.dma_start(out=cache, in_=dst),
                      "d_b%d" % slot_i, reads=slot.b())
        return slot

    def wsrc(w_ap2d, r0, nk, c0, ncols):
        return w_ap2d[r0:r0 + nk * 128, c0:c0 + ncols].rearrange("(k p) c -> p k c", p=128)

    def fm_job(w2d, c0, ncols, rhs_reg, nkchunks, T, row0=0):
        nm = ncols // 128
        banks = next_half()
        nkt = (nkchunks + 3) // 4
        roff = cur["C0"] if rhs_reg is hh else 0
        for kt in range(nkt):
            nk = min(4, nkchunks - kt * 4)
            slot = wfetch(wsrc(w2d, row0 + kt * 512, nk, c0, ncols), nk, ncols)

            def fn(e, slot=slot, kt=kt, nk=nk):
                ins = None
                for m in range(nm):
                    for k in range(nk):
                        kk = kt * 4 + k
                        ins = e.matmul(banks[m].ap[:, 0:T], slot.ap[:, k, m * 128:(m + 1) * 128],
                                       rhs_reg.ap[:, kk, roff:roff + T],
                                       start=(kk == 0), stop=(kk == nkchunks - 1))
                return ins
            S.op("pe", fn, reads=slot.b() + rhs_reg.b(kt * 4, kt * 4 + nk),
                 writes=[bk.bufs[0] for bk in banks[:nm]])
        return banks[:nm]

    def act(fn, reads, writes):
        S.op("act", fn, reads, writes)

    def dve(fn, reads, writes):
        S.op("dve", fn, reads, writes)

    S.dma("sp", lambda e: e.dma_start(out=consts_t[:, :, :], in_=consts_d[:, :, :]), msem(), writes=consts.b())
    S.dma("sp", lambda e: e.dma_start(out=cvec_t[:, :], in_=cvec_d[:, :]), msem(), writes=cvec.b())
    dve(lambda e: e.memset(xbh_t[:, :, :], 0.0), [], xbh.b())
    dve(lambda e: e.memset(zh_t[:, :, :], 0.0), [], zh.b())
    AR.reset()
    wst = AR.alloc([L * c.AG, 128], F32)
    S.dma("sp", lambda e: e.dma_start(out=wst.ap, in_=wsT_d.rearrange("l g s t -> s (l g) t")), msem(),
          writes=wst.b())
    for lg in range(L * c.AG):
        dve(lambda e, lg=lg: e.tensor_tensor(out=wmT_t[:, lg, :], in0=wst.ap[:, lg, :], in1=cmask, op=ALU.mult),
            wst.b() + consts.b(), wmT.b())

    def load_x(tok0, T):
        nt = T // 128
        AR.reset()
        stg = [AR.alloc([1024], F32) for _ in range(4)]
        nslab = D // 1024
        i = 0
        for tc in range(nt):
            for s_ in range(nslab):
                st = stg[i % 4]
                semn = "d_x%d" % (i % 4)
                S.dma("sp", lambda e, st=st, tc=tc, s_=s_: e.dma_start(
                    out=st.ap, in_=x_d[tok0 + tc * 128: tok0 + (tc + 1) * 128, s_ * 1024:(s_ + 1) * 1024]),
                    semn, writes=st.b())
                banks = next_half()
                for b2 in range(2):
                    def fn(e, st=st, b2=b2, banks=banks):
                        ins = None
                        for q in range(4):
                            ins = e.transpose(banks[b2].ap[:, q * 128:(q + 1) * 128],
                                              st.ap[:, (b2 * 4 + q) * 128:(b2 * 4 + q + 1) * 128], ident)
                        return ins
                    S.op("pe", fn, reads=st.b() + consts.b(), writes=banks[b2].b())
                    k0 = s_ * 8 + b2 * 4
                    src = banks[b2].ap[:, :].rearrange("p (a b) -> p a b", b=128)
                    dst = xs_t[:, k0:k0 + 4, tc * 128:(tc + 1) * 128]
                    if b2 == 0:
                        act(lambda e, dst=dst, src=src: e.activation(out=dst, in_=src, func=AF.Copy),
                            banks[b2].b(), xs.b(k0, k0 + 4))
                    else:
                        dve(lambda e, dst=dst, src=src: e.tensor_copy(out=dst, in_=src),
                            banks[b2].b(), xs.b(k0, k0 + 4))
                i += 1

    def rms_stats(T):
        sq = [AR.alloc([T], F32) for _ in range(8)]
        bank = next_half()[0]
        NG = KD // 4
        for g4 in range(NG):
            qs = sq[(g4 % 2) * 4:(g4 % 2) * 4 + 4]
            for i in range(4):
                k = g4 * 4 + i
                act(lambda e, q=qs[i], k=k: e.activation(out=q.ap, in_=xs_t[:, k, 0:T], func=AF.Square),
                    xs.b(k), qs[i].b())
            dve(lambda e, a=qs[0], b=qs[1]: e.tensor_tensor(out=a.ap, in0=a.ap, in1=b.ap, op=ALU.add),
                qs[0].b() + qs[1].b(), qs[0].b())
            dve(lambda e, a=qs[2], b=qs[3]: e.tensor_tensor(out=a.ap, in0=a.ap, in1=b.ap, op=ALU.add),
                qs[2].b() + qs[3].b(), qs[2].b())
            dve(lambda e, a=qs[0], b=qs[2]: e.tensor_tensor(out=a.ap, in0=a.ap, in1=b.ap, op=ALU.add),
                qs[0].b() + qs[2].b(), qs[0].b())
            S.op("pe", lambda e, q=qs[0], g4=g4: e.matmul(bank.ap[:, 0:T], ones, q.ap, start=(g4 == 0), stop=(g4 == NG - 1)),
                 reads=qs[0].b() + consts.b(), writes=bank.b())
        act(lambda e: e.activation(out=rstd_t[:, 0:T], in_=bank.ap[:, 0:T], func=AF.Sqrt, scale=1.0 / D, bias=EPS),
            bank.b(), rstd.b())
        dve(lambda e: e.reciprocal(out=rstd_t[:, 0:T], in_=rstd_t[:, 0:T]), rstd.b(), rstd.b())

    def rmsnorm_to_h(T, gcol0):
        rms_stats(T)
        for k in range(KD):
            dve(lambda e, k=k: e.scalar_tensor_tensor(out=h_t[:, k, 0:T], in0=xs_t[:, k, 0:T],
                                                      scalar=cvec_t[:, gcol0 + k:gcol0 + k + 1],
                                                      in1=rstd_t[:, 0:T], op0=ALU.mult, op1=ALU.mult),
                xs.b(k) + cvec.b() + rstd.b(), hh.b(k))

    def branch_a(l, T):
        nt = T // 128
        NFG = c.DA // 512
        win = w_in_d[l]
        C0 = cur["C0"]
        Tj = T - C0
        AR.reset()
        y_a = AR.alloc([c.KA, Tj], BF16, per_chunk=True)
        mark = AR.off
        v = AR.alloc([nt, c.DA], BF16, per_chunk=True)
        tmp = AR.alloc([c.DA], F32)
        lng = AR.alloc([c.DA], F32)
        lnb = AR.alloc([c.DA], F32)
        bsb = AR.alloc([c.AG, 128], F32)
        u_sb = AR.alloc([4, Tj], F32, per_chunk=True)
        tmp2s = [AR.alloc([T], F32), AR.alloc([T], F32)]
        S.dma("sp", lambda e: e.dma_start(out=lng.ap, in_=lng_d[l].partition_broadcast(128)), msem(), writes=lng.b())
        S.dma("sp", lambda e: e.dma_start(out=lnb.ap, in_=lnb_d[l].partition_broadcast(128)), msem(), writes=lnb.b())
        S.dma("sp", lambda e: e.dma_start(out=bsb.ap.rearrange("p a b -> p (a b)"),
                                          in_=bsp_d[l].partition_broadcast(128)), msem(), writes=bsb.b())
        if c.sub < 1:
            return y_a, mark
        for fg in range(NFG):
            banks = next_half()
            c0 = c.V0 + fg * 512
            for kt in range(KD // 4):
                slot = wfetch(wsrc(win, kt * 512, 4, c0, 512), 4, 512)

                def fn(e, slot=slot, kt=kt, banks=banks):
                    ins = None
                    for tc in range(nt):
                        for k in range(4):
                            kk = kt * 4 + k
                            ins = e.matmul(banks[tc].ap[:, :], h_t[:, kk, tc * 128:(tc + 1) * 128], slot.ap[:, k, :],
                                           start=(kk == 0), stop=(kk == KD - 1))
                    return ins
                S.op("pe", fn, reads=slot.b() + hh.b(kt * 4, kt * 4 + 4), writes=[bk.bufs[0] for bk in banks[:nt]])
            if c.sub < 2:
                continue
            for tc in range(nt):
                col = tc * NFG + fg
                vdst = v.ap[:, tc, fg * 512:(fg + 1) * 512]
                act(lambda e, tc=tc, vdst=vdst, banks=banks: e.activation(
                    out=vdst, in_=banks[tc].ap[:, :], func=AF.Gelu_apprx_tanh), banks[tc].b(), v.b(tc))
                dve(lambda e, vdst=vdst, col=col: e.tensor_reduce(out=stat_t[:, col:col + 1], in_=vdst,
                                                                  axis=AX.X, op=ALU.add), v.b(tc), statA.b())
                dve(lambda e, vdst=vdst, col=col: e.tensor_tensor(out=tmp.ap[:, 0:512], in0=vdst, in1=vdst,
                                                                  op=ALU.mult), v.b(tc), tmp.b())
                dve(lambda e, col=col: e.tensor_reduce(out=stat_t[:, 16 + col:17 + col], in_=tmp.ap[:, 0:512],
                                                       axis=AX.X, op=ALU.add), tmp.b(), stat.b())
        if c.sub < 3:
            return y_a, mark
        NS = nt * NFG
        dve(lambda e: e.tensor_reduce(out=stat_t[:, 32:32 + nt],
                                      in_=stat_t[:, 0:NS].rearrange("p (a b) -> p a b", b=NFG),
                                      axis=AX.X, op=ALU.add), statA.b(), stat.b())
        dve(lambda e: e.tensor_reduce(out=stat_t[:, 36:36 + nt],
                                      in_=stat_t[:, 16:16 + NS].rearrange("p (a b) -> p a b", b=NFG),
                                      axis=AX.X, op=ALU.add), stat.b(), stat.b())
        dve(lambda e: e.tensor_scalar(out=stat_t[:, 40:40 + nt], in0=stat_t[:, 32:32 + nt], scalar1=1.0 / c.DA,
                                      scalar2=None, op0=ALU.mult), stat.b(), stat.b())
        dve(lambda e: e.tensor_tensor(out=stat_t[:, 44:44 + nt], in0=stat_t[:, 40:40 + nt], in1=stat_t[:, 40:40 + nt],
                                      op=ALU.mult), stat.b(), stat.b())
        dve(lambda e: e.scalar_tensor_tensor(out=stat_t[:, 44:44 + nt], in0=stat_t[:, 36:36 + nt], scalar=1.0 / c.DA,
                                             in1=stat_t[:, 44:44 + nt], op0=ALU.mult, op1=ALU.subtract),
            stat.b(), stat.b())
        act(lambda e: e.activation(out=stat_t[:, 48:48 + nt], in_=stat_t[:, 44:44 + nt], func=AF.Sqrt, scale=1.0, bias=EPS),
            stat.b(), stat.b())
        dve(lambda e: e.reciprocal(out=stat_t[:, 48:48 + nt], in_=stat_t[:, 48:48 + nt]), stat.b(), stat.b())
        if c.sub < 4:
            return y_a, mark
        for tc in range(nt):
            dve(lambda e, tc=tc: e.tensor_scalar(out=tmp.ap, in0=v.ap[:, tc, :], scalar1=stat_t[:, 40 + tc:41 + tc],
                                                 scalar2=stat_t[:, 48 + tc:49 + tc], op0=ALU.subtract, op1=ALU.mult),
                v.b(tc) + stat.b(), tmp.b())
            if c.sub < 6:
                continue
            dve(lambda e: e.tensor_tensor(out=tmp.ap, in0=tmp.ap, in1=lng.ap, op=ALU.mult), tmp.b() + lng.b(), tmp.b())
            if c.sub < 7:
                continue
            dve(lambda e, tc=tc: e.tensor_tensor(out=v.ap[:, tc, :], in0=tmp.ap, in1=lnb.ap, op=ALU.add),
                tmp.b() + lnb.b(), v.b(tc))
        if c.sub < 8:
            return y_a, mark
        for cgu in range(c.DA // 512):
            banks = fm_job(win, c.U0 + cgu * 512, 512, hh, KD, Tj)
            for m in range(4):
                act(lambda e, m=m, banks=banks: e.activation(out=u_sb.ap[:, m, :], in_=banks[m].ap[:, 0:Tj],
                                                             func=AF.Gelu_apprx_tanh), banks[m].b(), u_sb.b(m))
            mb = next_half()
            for m in range(4):
                j = cgu * 4 + m
                g = j // 2

                def fn(e, m=m, j=j, g=g, mb=mb):
                    ins = None
                    for tc in range(nt):
                        ins = e.matmul(mb[m].ap[:, tc * 128:(tc + 1) * 128], v.ap[:, tc, j * 128:(j + 1) * 128],
                                       wmT_t[:, l * c.AG + g, :], start=True, stop=True)
                    return ins
                S.op("pe", fn, reads=v.b() + wmT.b(), writes=mb[m].b())
                t2 = tmp2s[m % 2]
                dve(lambda e, m=m, g=g, mb=mb, t2=t2: e.tensor_tensor(
                    out=t2.ap.rearrange("p (a b) -> p a b", b=128),
                    in0=mb[m].ap[:, 0:T].rearrange("p (a b) -> p a b", b=128),
                    in1=bsb.ap[:, g:g + 1, :].to_broadcast([128, nt, 128]), op=ALU.add),
                    mb[m].b() + bsb.b(), t2.b())
                dve(lambda e, m=m, j=j, t2=t2: e.tensor_tensor(out=y_a.ap[:, j, :], in0=t2.ap[:, C0:T], in1=u_sb.ap[:, m, :],
                                                               op=ALU.mult), t2.b() + u_sb.b(m), y_a.b(j))
        return y_a, mark

    def branch_b(l, T, y_b, use_tab, halo_only):
        win = w_in_d[l]
        CB = c.CB
        W16 = T + 16
        xbuf = [AR.alloc([W16], F32) for _ in range(4)]
        pa = [AR.alloc([W16], F32) for _ in range(4)]
        pb = [AR.alloc([W16], F32) for _ in range(4)]
        pooled = [AR.alloc([CB, T], BF16, per_chunk=True) for _ in range(4)]
        tab = None
        if use_tab and not halo_only:
            tab = AR.alloc([4, T], F32)
            S.dma("sp", lambda e, C0=cur["C0"]: e.dma_start(out=tab.ap, in_=invc_d[:, :, C0:C0 + T]), msem(), writes=tab.b())
        pend = []
        for jj in range(c.DB // 512):
            while len(pend) > 1:
                pend.pop(0)()
            banks = fm_job(win, c.XB0 + jj * 512, 512, hh, KD, T)
            if pend:
                pend.pop(0)()
            for m in range(4):
                j = jj * 4 + m
                xb_ = xbuf[m]
                hidx = l * c.KB + j
                act(lambda e, xb_=xb_, hidx=hidx: e.activation(out=xb_.ap[:, 0:16], in_=xbh_t[:, hidx, :], func=AF.Copy),
                    xbh.b(hidx), xb_.b())
                act(lambda e, xb_=xb_, m=m, banks=banks: e.activation(out=xb_.ap[:, 16:16 + T], in_=banks[m].ap[:, 0:T],
                                                                    func=AF.Copy), banks[m].b(), xb_.b())
                act(lambda e, xb_=xb_, hidx=hidx: e.activation(out=xbh_t[:, hidx, :], in_=xb_.ap[:, T:T + 16], func=AF.Copy),
                    xb_.b(), xbh.b(hidx))
            if halo_only:
                continue
            srcs = [xbuf[m] for m in range(4)]
            los = [0, 0, 0, 0]
            gs = [(jj * 4 + m) // CB for m in range(4)]
            for stp in range(max(gs) + 1):
                for m in range(4):
                    if stp > gs[m]:
                        continue
                    sh = 1 << stp
                    dst = pa[m] if stp % 2 == 0 else pb[m]
                    src = srcs[m]
                    lo2 = los[m] + sh
                    dve(lambda e, src=src, dst=dst, lo2=lo2, sh=sh: e.tensor_tensor(
                        out=dst.ap[:, lo2:W16], in0=src.ap[:, lo2:W16], in1=src.ap[:, lo2 - sh:W16 - sh], op=ALU.add),
                        src.b(), dst.b())
                    srcs[m] = dst
                    los[m] = lo2
            if tab is not None:
                for m in range(4):
                    dve(lambda e, src=srcs[m], g=gs[m]: e.tensor_tensor(out=src.ap[:, 16:W16], in0=src.ap[:, 16:W16],
                                                                        in1=tab.ap[:, g, :], op=ALU.mult),
                        srcs[m].b() + tab.b(), srcs[m].b())
            for m in range(4):
                j = jj * 4 + m
                g = gs[m]
                src = srcs[m]
                xb_ = xbuf[m]
                pl = pooled[g % 4]
                wn = float(1 << (g + 1))
                if tab is None:
                    dve(lambda e, src=src, xb_=xb_, pl=pl, j=j, wn=wn: e.scalar_tensor_tensor(
                        out=pl.ap[:, j % CB, :], in0=src.ap[:, 16:W16], scalar=1.0 / wn, in1=xb_.ap[:, 16:W16],
                        op0=ALU.mult, op1=ALU.subtract), src.b() + xb_.b(), pl.b(j % CB))
                else:
                    dve(lambda e, src=src, xb_=xb_, pl=pl, j=j: e.tensor_tensor(
                        out=pl.ap[:, j % CB, :], in0=src.ap[:, 16:W16], in1=xb_.ap[:, 16:W16], op=ALU.subtract),
                        src.b() + xb_.b(), pl.b(j % CB))
                if (j + 1) % CB == 0:
                    def emit_pool(g=g, pl=pl):
                        pbanks = next_half()
                        job_ctr[0] -= 1
                        slot = wfetch(wsrc(w_pool_d[l, g], 0, CB, 0, CB * 128), CB, CB * 128)

                        def fn(e, slot=slot, pl=pl, pbanks=pbanks):
                            ins = None
                            for mm in range(CB):
                                for k in range(CB):
                                    ins = e.matmul(pbanks[mm].ap[:, 0:T], slot.ap[:, k, mm * 128:(mm + 1) * 128],
                                                   pl.ap[:, k, :], start=(k == 0), stop=(k == CB - 1))
                            return ins
                        S.op("pe", fn, reads=slot.b() + pl.b(), writes=[bk.bufs[0] for bk in pbanks[:CB]])
                        for mm in range(CB):
                            jo = g * CB + mm
                            col = l * c.CV_L + c.CV_PSC + jo
                            dve(lambda e, mm=mm, jo=jo, col=col, pbanks=pbanks: e.tensor_scalar(
                                out=y_b.ap[:, jo, :], in0=pbanks[mm].ap[:, 0:T], scalar1=cvec_t[:, col:col + 1],
                                scalar2=None, op0=ALU.mult), pbanks[mm].b() + cvec.b(), y_b.b(jo))
                    if os.environ.get('KNODEFER', '0') == '1':
                        emit_pool()
                    else:
                        pend.append(emit_pool)
        while len(pend) > 1:
            pend.pop(0)()
        return pend

    def branch_c(l, T, y_c, halo_only, pre_jobs=()):
        pre_jobs = list(pre_jobs)
        win = w_in_d[l]
        cc_sb = AR.alloc([4, T], F32, per_chunk=True)
        z = AR.alloc([4, T + 2], F32, per_chunk=True)
        acc = AR.alloc([4, T], F32, per_chunk=True)
        for jj in range(c.DC // 512):
            banks = fm_job(win, c.CC0 + jj * 512, 512, hh, KD, T)
            while pre_jobs:
                pre_jobs.pop(0)()
            for m in range(4):
                act(lambda e, m=m, banks=banks: e.activation(out=cc_sb.ap[:, m, :], in_=banks[m].ap[:, 0:T], func=AF.Copy),
                    banks[m].b(), cc_sb.b(m))
            banks = fm_job(win, c.HC0 + jj * 512, 512, hh, KD, T)
            for m in range(4):
                j = jj * 4 + m
                hidx = l * c.KC + j
                act(lambda e, m=m, hidx=hidx: e.activation(out=z.ap[:, m, 0:2], in_=zh_t[:, hidx, :], func=AF.Copy),
                    zh.b(hidx), z.b(m))
                dve(lambda e, m=m, banks=banks: e.tensor_tensor(out=z.ap[:, m, 2:2 + T], in0=cc_sb.ap[:, m, :],
                                                                in1=banks[m].ap[:, 0:T], op=ALU.mult),
                    cc_sb.b(m) + banks[m].b(), z.b(m))
                act(lambda e, m=m, hidx=hidx: e.activation(out=zh_t[:, hidx, :], in_=z.ap[:, m, T:T + 2], func=AF.Copy),
                    z.b(m), zh.b(hidx))
                if halo_only:
                    continue
                cw = l * c.CV_L + c.CV_CONV
                dve(lambda e, m=m, j=j, cw=cw: e.tensor_scalar(
                    out=acc.ap[:, m, :], in0=z.ap[:, m, 2:2 + T], scalar1=cvec_t[:, cw + 2 * c.KC + j:cw + 2 * c.KC + j + 1],
                    scalar2=None, op0=ALU.mult), z.b(m) + cvec.b(), acc.b(m))
                dve(lambda e, m=m, j=j, cw=cw: e.scalar_tensor_tensor(
                    out=acc.ap[:, m, :], in0=z.ap[:, m, 1:1 + T], scalar=cvec_t[:, cw + c.KC + j:cw + c.KC + j + 1],
                    in1=acc.ap[:, m, :], op0=ALU.mult, op1=ALU.add), z.b(m) + cvec.b() + acc.b(m), acc.b(m))
                dve(lambda e, m=m, j=j, cw=cw: e.scalar_tensor_tensor(
                    out=acc.ap[:, m, :], in0=z.ap[:, m, 0:T], scalar=cvec_t[:, cw + j:cw + j + 1],
                    in1=acc.ap[:, m, :], op0=ALU.mult, op1=ALU.add), z.b(m) + cvec.b() + acc.b(m), acc.b(m))
            if halo_only:
                continue
            banks = fm_job(win, c.BC0 + jj * 512, 512, hh, KD, T)
            for m in range(4):
                j = jj * 4 + m
                dve(lambda e, m=m, j=j, banks=banks: e.tensor_tensor(out=y_c.ap[:, j, :], in0=banks[m].ap[:, 0:T],
                                                                    in1=acc.ap[:, m, :], op=ALU.mult),
                    banks[m].b() + acc.b(m), y_c.b(j))

    def merge_and_out(l, T, ys):
        win = w_in_d[l]
        sg = AR.alloc([4, T], F32, per_chunk=True)
        acc = AR.alloc([4, T], F32, per_chunk=True)
        mg = [AR.alloc([4, T], BF16, per_chunk=True), AR.alloc([4, T], BF16, per_chunk=True)]
        pend_out = []

        def emit_out(cg):
            for n in range(D // 512):
                banks = fm_job(w_out_d[l], n * 512, 512, mg[cg % 2], 4, T, row0=cg * 512)
                for m in range(4):
                    kx = n * 4 + m
                    dve(lambda e, m=m, kx=kx, banks=banks, C0=cur["C0"]: e.tensor_tensor(
                        out=xs_t[:, kx, C0:C0 + T], in0=xs_t[:, kx, C0:C0 + T], in1=banks[m].ap[:, 0:T], op=ALU.add),
                        xs.b(kx) + banks[m].b(), xs.b(kx))

        for cg in range(D // 512):
            for br in range(3):
                banks = fm_job(win, c.G0 + br * D + cg * 512, 512, hh, KD, T)
                for m in range(4):
                    act(lambda e, m=m, banks=banks: e.activation(out=sg.ap[:, m, :], in_=banks[m].ap[:, 0:T],
                                                                 func=AF.Sigmoid), banks[m].b(), sg.b(m))
                if br == 0 and pend_out:
                    emit_out(pend_out.pop(0))
                banks = fm_job(w_br_d[br][l], cg * 512, 512, ys[br], c.KA, T)
                for m in range(4):
                    if br == 0:
                        dve(lambda e, m=m, banks=banks: e.tensor_tensor(out=acc.ap[:, m, :], in0=sg.ap[:, m, :],
                                                                        in1=banks[m].ap[:, 0:T], op=ALU.mult),
                            sg.b(m) + banks[m].b(), acc.b(m))
                    else:
                        dve(lambda e, m=m, banks=banks: e.tensor_tensor(out=sg.ap[:, m, :], in0=sg.ap[:, m, :],
                                                                        in1=banks[m].ap[:, 0:T], op=ALU.mult),
                            sg.b(m) + banks[m].b(), sg.b(m))
                        if br == 1:
                            dve(lambda e, m=m: e.tensor_tensor(out=acc.ap[:, m, :], in0=acc.ap[:, m, :],
                                                               in1=sg.ap[:, m, :], op=ALU.add),
                                sg.b(m) + acc.b(m), acc.b(m))
                        else:
                            dve(lambda e, m=m, mgc=mg[cg % 2]: e.tensor_tensor(out=mgc.ap[:, m, :], in0=acc.ap[:, m, :],
                                                                              in1=sg.ap[:, m, :], op=ALU.add),
                                sg.b(m) + acc.b(m), mg[cg % 2].b(m))
            pend_out.append(cg)
        while pend_out:
            emit_out(pend_out.pop(0))

    def ffn(l, T):
        AR.reset()
        halves = [(0, (c.KF + 1) // 2), ((c.KF + 1) // 2, c.KF)]
        ff = AR.alloc([(c.KF + 1) // 2, T], BF16, per_chunk=True)
        sgf = AR.alloc([4, T], F32, per_chunk=True)
        for (f0, f1) in halves:
            nf = f1 - f0
            fi = 0
            while fi < nf:
                nm = min(4, nf - fi)
                c0 = (f0 + fi) * 128
                banks = fm_job(w_gate_d[l], c0, nm * 128, hh, KD, T)
                for m in range(nm):
                    act(lambda e, m=m, banks=banks: e.activation(out=sgf.ap[:, m, :], in_=banks[m].ap[:, 0:T],
                                                                 func=AF.Silu), banks[m].b(), sgf.b(m))
                banks = fm_job(w_up_d[l], c0, nm * 128, hh, KD, T)
                for m in range(nm):
                    dve(lambda e, m=m, fi=fi, banks=banks: e.tensor_tensor(out=ff.ap[:, fi + m, :], in0=sgf.ap[:, m, :],
                                                                          in1=banks[m].ap[:, 0:T], op=ALU.mult),
                        sgf.b(m) + banks[m].b(), ff.b(fi + m))
                fi += nm
            for n in range(D // 512):
                banks = fm_job(w_down_d[l], n * 512, 512, ff, nf, T, row0=f0 * 128)
                for m in range(4):
                    kx = n * 4 + m
                    dve(lambda e, m=m, kx=kx, banks=banks, C0=cur["C0"]: e.tensor_tensor(
                        out=xs_t[:, kx, C0:C0 + T], in0=xs_t[:, kx, C0:C0 + T], in1=banks[m].ap[:, 0:T], op=ALU.add),
                        xs.b(kx) + banks[m].b(), xs.b(kx))

    def final_store(tok0_, T, skip):
        nt = T // 128
        tok_out0 = tok0_ - c.NPRE
        AR.reset()
        rms_stats(T)
        osl = AR.alloc([8, T], F32, per_chunk=True)
        ost = [AR.alloc([1024], F32), AR.alloc([1024], F32)]
        i = 0
        for s_ in range(D // 1024):
            for k in range(8):
                kk = s_ * 8 + k
                dve(lambda e, k=k, kk=kk: e.scalar_tensor_tensor(out=osl.ap[:, k, :], in0=xs_t[:, kk, 0:T],
                                                                 scalar=cvec_t[:, c.CV_FIN + kk:c.CV_FIN + kk + 1],
                                                                 in1=rstd_t[:, 0:T], op0=ALU.mult, op1=ALU.mult),
                    xs.b(kk) + cvec.b() + rstd.b(), osl.b(k))
            for tc in range(skip, nt):
                st = ost[i % 2]
                semn = "d_o%d" % (i % 2)
                banks = next_half()
                for b2 in range(2):
                    def fn(e, b2=b2, tc=tc, banks=banks):
                        ins = None
                        for q in range(4):
                            ins = e.transpose(banks[b2].ap[:, q * 128:(q + 1) * 128],
                                              osl.ap[:, b2 * 4 + q, tc * 128:(tc + 1) * 128], ident)
                        return ins
                    S.op("pe", fn, reads=osl.b(b2 * 4, b2 * 4 + 4) + consts.b(), writes=banks[b2].b())
                    if b2 == 0:
                        act(lambda e, st=st, banks=banks: e.activation(out=st.ap[:, 0:512], in_=banks[0].ap[:, :],
                                                                       func=AF.Copy), banks[0].b(), st.b())
                    else:
                        dve(lambda e, st=st, banks=banks: e.tensor_copy(out=st.ap[:, 512:1024], in_=banks[1].ap[:, :]),
                            banks[1].b(), st.b())
                S.dma("sp", lambda e, st=st, tc=tc, s_=s_: e.dma_start(
                    out=y_d[tok_out0 + tc * 128: tok_out0 + (tc + 1) * 128, s_ * 1024:(s_ + 1) * 1024], in_=st.ap),
                    semn, reads=st.b())
                i += 1

    STAGES = ("load", "norm", "A", "B", "C", "M", "ffn", None)

    def upto(name):
        return STAGES.index(c.stop) >= STAGES.index(name)

    def layer(l, T, first_tile):
        cv = l * c.CV_L
        AR.reset()
        if not upto("norm"):
            return
        rmsnorm_to_h(T, cv + c.CV_GMIX)
        if not upto("A"):
            return
        y_a, mark = branch_a(l, T)
        Tfull = T
        T = T - cur["C0"]
        AR.off = mark
        y_b = AR.alloc([c.KB, T], BF16, per_chunk=True)
        mark_b = AR.off
        if not upto("B"):
            return
        pend_b = branch_b(l, T, y_b, first_tile, False)
        AR.off = mark_b
        y_c = AR.alloc([c.KC, T], BF16, per_chunk=True)
        mark_c = AR.off
        if not upto("C"):
            for f_ in pend_b:
                f_()
            return
        branch_c(l, T, y_c, False, pend_b)
        AR.off = mark_c
        if not upto("M"):
            return
        merge_and_out(l, T, [y_a, y_b, y_c])
        AR.reset()
        if not upto("ffn"):
            return
        rmsnorm_to_h(Tfull, cv + c.CV_GFFN)
        ffn(l, T)

    tok0 = 0
    for ti, T in enumerate(c.tiles):
        cur["ti"] = ti
        cur["C0"] = c.C0 if ti == 0 else 0
        if ti >= 1:
            S.pending["pool"] = [("d_b%d" % i, S.cnt["d_b%d" % i]) for i in range(c.nslot) if ("d_b%d" % i) in S.cnt]
        load_x(tok0, T)
        for l in range(L):
            cur["l"] = l
            cur["seq"] = 0
            layer(l, T, ti == 0)
        final_store(tok0, T, 1 if ti == 0 else 0)
        tok0 += T

    final_waits = [(s_, S.cnt[s_]) for s_ in ("d_o0", "d_o1") if s_ in S.cnt]

    sem_ctx = {n: es.enter_context(nc.semaphore(n)) for n in sem_names}
    block = es.enter_context(nc.Block())

    def replay(engname, e):
        for waits, fn, sem, inc in S.ops[engname]:
            for (s_, v) in waits:
                e.wait_ge(sem_ctx[s_], v)
            ins = fn(e)
            ins.then_inc(sem_ctx[sem], inc)

    @block.tensor
    def _(e):
        replay("pe", e)

    @block.scalar
    def _(e):
        replay("act", e)

    @block.vector
    def _(e):
        replay("dve", e)

    @block.gpsimd
    def _(e):
        replay("pool", e)

    @block.sync
    def _(e):
        replay("sp", e)
        for (s_, v) in final_waits:
            e.wait_ge(sem_ctx[s_], v)

    es.close()
    return nc, S


def host_inputs(cfg, inp, core):
    c = cfg
    x = inp["x"].reshape(-1, c.D)
    per = c.NTOK
    lo = core * per - c.NPRE
    if lo < 0:
        xc = np.concatenate([np.zeros((c.NPRE, c.D), np.float32), x[0:per]], axis=0)
    else:
        xc = np.ascontiguousarray(x[lo:lo + c.NPRE + per])
    m = {"x": xc}
    return m


def shared_inputs(cfg, inp):
    c = cfg
    L = c.L
    sh = {}
    for k in ("w_in", "w_pool", "w_branch_a", "w_branch_b", "w_branch_c", "w_out",
              "w_ffn_gate", "w_ffn_up", "w_ffn_down", "ln_a_g", "ln_a_b"):
        sh[k] = np.ascontiguousarray(inp[k], dtype=np.float32)
    sh["wsT"] = np.ascontiguousarray(np.transpose(inp["w_spatial"], (0, 1, 3, 2)))
    sh["b_spatial"] = np.ascontiguousarray(inp["b_spatial"].reshape(L, -1))
    cv = np.zeros((128, c.NCV), np.float32)
    for l in range(L):
        b = l * c.CV_L
        cv[:, b + c.CV_GMIX:b + c.CV_GMIX + c.KD] = inp["norm_mix_g"][l].reshape(c.KD, 128).T
        cv[:, b + c.CV_GFFN:b + c.CV_GFFN + c.KD] = inp["norm_ffn_g"][l].reshape(c.KD, 128).T
        cv[:, b + c.CV_PSC:b + c.CV_PSC + c.KB] = inp["pool_scale"][l].reshape(c.KB, 128).T
        for j in range(3):
            cv[:, b + c.CV_CONV + j * c.KC:b + c.CV_CONV + (j + 1) * c.KC] = inp["conv_w"][l, j].reshape(c.KC, 128).T
    cv[:, c.CV_FIN:c.CV_FIN + c.KD] = inp["final_norm_g"].reshape(c.KD, 128).T
    sh["cvec"] = cv
    cs = np.zeros((128, 3, 128), np.float32)
    cs[:, 0, :] = np.eye(128, dtype=np.float32)
    cs[:, 1, :] = np.triu(np.ones((128, 128), np.float32))
    cs[:, 2, :] = 1.0
    sh["consts"] = cs
    return sh


def invc_table(cfg, core):
    c = cfg
    T0 = c.tiles[0]
    pos = core * c.NTOK + np.arange(1 - c.NPRE, T0 - c.NPRE + 1, dtype=np.float32)
    pos = np.maximum(pos, 1.0)
    tab = np.zeros((128, 4, T0), np.float32)
    for g in range(4):
        win = float(2 ** (g + 1))
        tab[:, g, :] = (np.float32(1.0) / np.minimum(pos, win))[None, :]
    return tab


_CACHE = {}


def run(cfg, inp):
    key = (cfg.D, cfg.DFF, cfg.tiles, cfg.ncores)
    if key not in _CACHE:
        _CACHE[key] = build(cfg)[0]
    nc = _CACHE[key]
    sh = shared_inputs(cfg, inp)
    in_maps = []
    for core in range(cfg.ncores):
        m = dict(sh)
        m.update(host_inputs(cfg, inp, core))
        m["invc"] = invc_table(cfg, core)
        in_maps.append(m)
    res = run_bass_kernel_spmd(nc, in_maps, core_ids=list(range(cfg.ncores)))
    out = np.concatenate([r["y"] for r in res.results], axis=0)
    return out


def kernel(**inputs):
    cfg = Cfg()
    inp = {k: np.asarray(v) for k, v in inputs.items()}
    out = run(cfg, inp)
    return out.reshape(1, cfg.ncores * cfg.NTOK, cfg.D).astype(np.float32, copy=False)
```
